# Optimizing a Trainium2 kernel written in Bass

```python
import math
import jax, jax.numpy as jnp
from jax import lax
import numpy as np

D_MODEL = 1024
BATCH = 8
SEQ = 4096
DEPTH = 4

BRANCH_WIDTH = 512
N_BRANCHES = 3
SWA_Q_HEADS = 8
SWA_KV_HEADS = 2
SWA_GROUP = SWA_Q_HEADS // SWA_KV_HEADS
SWA_HEAD_DIM = BRANCH_WIDTH // SWA_Q_HEADS
WINDOW = 128
N_BUCKETS = 32
MAX_DISTANCE = 128
CONV_CHANNELS = BRANCH_WIDTH
CONV_WIDTH = 31
MLA_HEADS = 8
MLA_Q_RANK = 256
MLA_KV_RANK = 128
MLA_NOPE_DIM = 64
MLA_ROPE_DIM = 32
MLA_V_DIM = BRANCH_WIDTH // MLA_HEADS
ROPE_THETA = 10000.0
Q_BLOCK = 128
D_FF = 4 * D_MODEL
EPS = 1e-6
NEG_INF = -1e30

IN_SPLIT_SIZES = (
    SWA_Q_HEADS * SWA_HEAD_DIM,
    SWA_KV_HEADS * SWA_HEAD_DIM,
    SWA_KV_HEADS * SWA_HEAD_DIM,
    2 * CONV_CHANNELS,
    MLA_Q_RANK,
    MLA_KV_RANK,
    MLA_ROPE_DIM,
    N_BRANCHES * D_MODEL,
)
IN_COLS = sum(IN_SPLIT_SIZES)
IN_SPLIT_POINTS = tuple(sum(IN_SPLIT_SIZES[: i + 1]) for i in range(len(IN_SPLIT_SIZES) - 1))

kernel_name = "hybrid_swa_conformer_mla_gated_block"


def rms_norm(x, g):
    xf = x.astype(jnp.float32)
    y = xf * lax.rsqrt(jnp.mean(xf * xf, axis=-1, keepdims=True) + EPS)
    return (y * g.astype(jnp.float32)).astype(x.dtype)


def layer_norm(x, g, b):
    xf = x.astype(jnp.float32)
    mu = jnp.mean(xf, axis=-1, keepdims=True)
    var = jnp.mean(jnp.square(xf - mu), axis=-1, keepdims=True)
    y = (xf - mu) * lax.rsqrt(var + EPS) * g.astype(jnp.float32) + b.astype(jnp.float32)
    return y.astype(x.dtype)


def t5_bucket(rel):
    n = jnp.maximum(rel, 0)
    max_exact = N_BUCKETS // 2
    nf = jnp.maximum(n, max_exact).astype(jnp.float32)
    large = max_exact + (jnp.log(nf / max_exact) / math.log(MAX_DISTANCE / max_exact)
                         * (N_BUCKETS - max_exact)).astype(jnp.int32)
    return jnp.where(n < max_exact, n, jnp.minimum(large, N_BUCKETS - 1))


def band_frames(t):
    b, s = t.shape[:2]
    blocks = t.reshape((b, s // WINDOW, WINDOW) + t.shape[2:])
    prev = jnp.concatenate([jnp.zeros_like(blocks[:, :1]), blocks[:, :-1]], axis=1)
    return jnp.concatenate([prev, blocks], axis=2)


def swa_bias_and_mask(positions, rel_bias):
    b, s = positions.shape
    nb = s // WINDOW
    pq = positions.reshape(b, nb, WINDOW)
    pk = band_frames(positions)
    bucket = t5_bucket(pq[..., :, None] - pk[..., None, :])
    bias = rel_bias[bucket].astype(jnp.float32).reshape(b, nb, WINDOW, 2 * WINDOW, SWA_KV_HEADS, SWA_GROUP)
    qi = WINDOW + jnp.arange(WINDOW)[:, None]
    ki = jnp.arange(2 * WINDOW)[None, :]
    in_band = (ki <= qi) & (qi - ki < WINDOW)
    has_prev = (jnp.arange(nb) > 0)[:, None, None] | (ki >= WINDOW)[None]
    mask = in_band[None] & has_prev
    return bias, mask[None, :, :, :, None, None]


def swa_attention(q, k, v, sinks, bias, mask):
    b, s = q.shape[:2]
    nb = s // WINDOW
    qb = q.reshape(b, nb, WINDOW, SWA_KV_HEADS, SWA_GROUP, SWA_HEAD_DIM)
    kk = band_frames(k.reshape(b, s, SWA_KV_HEADS, SWA_HEAD_DIM))
    vv = band_frames(v.reshape(b, s, SWA_KV_HEADS, SWA_HEAD_DIM))
    scores = jnp.einsum('bnqhgd,bnkhd->bnqkhg', qb, kk).astype(jnp.float32) * (SWA_HEAD_DIM ** -0.5) + bias
    scores = jnp.where(mask, scores, NEG_INF)
    sink = sinks.astype(jnp.float32).reshape(SWA_KV_HEADS, SWA_GROUP)
    m = jnp.maximum(scores.max(axis=3, keepdims=True), sink)
    p = jnp.exp(scores - m)
    p = p / (p.sum(axis=3, keepdims=True) + jnp.exp(sink - m))
    out = jnp.einsum('bnqkhg,bnkhd->bnqhgd', p.astype(vv.dtype), vv)
    return out.reshape(b, s, SWA_Q_HEADS * SWA_HEAD_DIM)


def conformer_conv(u_glu, w_dw, b_dw, g_ln, b_ln):
    a, gate = jnp.split(u_glu, 2, axis=-1)
    u = a * jax.nn.sigmoid(gate)
    u = lax.conv_general_dilated(u, w_dw, window_strides=(1,), padding=[(CONV_WIDTH - 1, 0)],
                                 dimension_numbers=('NWC', 'WIO', 'NWC'),
                                 feature_group_count=CONV_CHANNELS) + b_dw
    return jax.nn.silu(layer_norm(u, g_ln, b_ln))


def rope(x, positions):
    d = x.shape[-1]
    half = d // 2
    freqs = jnp.exp(-math.log(ROPE_THETA) * jnp.arange(half, dtype=jnp.float32) / half)
    ang = positions.astype(jnp.float32)[..., None] * freqs
    cos = jnp.cos(ang)[:, :, None, :]
    sin = jnp.sin(ang)[:, :, None, :]
    x1 = x[..., :half].astype(jnp.float32)
    x2 = x[..., half:].astype(jnp.float32)
    return jnp.concatenate([x1 * cos - x2 * sin, x2 * cos + x1 * sin], axis=-1).astype(x.dtype)


def mla_attention(cq, ckv, kpe_raw, positions, g_qn, w_uq, g_kvn, w_ukv):
    b, s = cq.shape[:2]
    q = (rms_norm(cq, g_qn) @ w_uq).reshape(b, s, MLA_HEADS, MLA_NOPE_DIM + MLA_ROPE_DIM)
    q_nope = q[..., :MLA_NOPE_DIM]
    q_pe = rope(q[..., MLA_NOPE_DIM:], positions)
    kv = (rms_norm(ckv, g_kvn) @ w_ukv).reshape(b, s, MLA_HEADS, MLA_NOPE_DIM + MLA_V_DIM)
    k_nope = kv[..., :MLA_NOPE_DIM]
    v = kv[..., MLA_NOPE_DIM:]
    k_pe = rope(kpe_raw[:, :, None, :], positions)[:, :, 0]
    scale = (MLA_NOPE_DIM + MLA_ROPE_DIM) ** -0.5
    nb = s // Q_BLOCK
    key_idx = jnp.arange(s)

    def to_blocks(t):
        return jnp.moveaxis(t.reshape((b, nb, Q_BLOCK) + t.shape[2:]), 1, 0)

    def attend_block(args):
        qn, qp, blk = args
        sc = (jnp.einsum('bqhd,bkhd->bhqk', qn, k_nope)
              + jnp.einsum('bqhd,bkd->bhqk', qp, k_pe)).astype(jnp.float32) * scale
        q_idx = blk * Q_BLOCK + jnp.arange(Q_BLOCK)
        sc = jnp.where(key_idx[None, :] <= q_idx[:, None], sc, NEG_INF)
        p = jax.nn.softmax(sc, axis=-1)
        return jnp.einsum('bhqk,bkhd->bqhd', p.astype(v.dtype), v)

    out = lax.map(attend_block, (to_blocks(q_nope), to_blocks(q_pe), jnp.arange(nb)))
    return jnp.moveaxis(out, 0, 1).reshape(b, s, MLA_HEADS * MLA_V_DIM)


def setup_inputs(seed: int = 0) -> dict:
    key = jax.random.key(seed)
    ks = jax.random.split(key, 20)
    f32 = jnp.float32

    def nrm(k, shape, scale):
        return jax.random.normal(k, shape, f32) * scale

    def gain(k, shape):
        return 1.0 + 0.05 * jax.random.normal(k, shape, f32)

    return {
        "x": nrm(ks[0], (BATCH, SEQ, D_MODEL), 1.0),
        "positions": jnp.broadcast_to(jnp.arange(SEQ, dtype=jnp.int32), (BATCH, SEQ)),
        "rel_bias": nrm(ks[1], (N_BUCKETS, SWA_Q_HEADS), 0.5),
        "g_final": gain(ks[2], (D_MODEL,)),
        "g_mix": gain(ks[3], (DEPTH, D_MODEL)),
        "w_in": nrm(ks[4], (DEPTH, D_MODEL, IN_COLS), D_MODEL ** -0.5),
        "swa_sinks": nrm(ks[5], (DEPTH, SWA_Q_HEADS), 0.5),
        "g_q_norm": gain(ks[6], (DEPTH, MLA_Q_RANK)),
        "w_q_up": nrm(ks[7], (DEPTH, MLA_Q_RANK, MLA_HEADS * (MLA_NOPE_DIM + MLA_ROPE_DIM)), MLA_Q_RANK ** -0.5),
        "g_kv_norm": gain(ks[8], (DEPTH, MLA_KV_RANK)),
        "w_kv_up": nrm(ks[9], (DEPTH, MLA_KV_RANK, MLA_HEADS * (MLA_NOPE_DIM + MLA_V_DIM)), MLA_KV_RANK ** -0.5),
        "w_dw": nrm(ks[10], (DEPTH, CONV_WIDTH, 1, CONV_CHANNELS), CONV_WIDTH ** -0.5),
        "b_dw": nrm(ks[11], (DEPTH, CONV_CHANNELS), 0.02),
        "g_conv_ln": gain(ks[12], (DEPTH, CONV_CHANNELS)),
        "b_conv_ln": nrm(ks[13], (DEPTH, CONV_CHANNELS), 0.02),
        "w_branch": nrm(ks[14], (DEPTH, N_BRANCHES, BRANCH_WIDTH, D_MODEL), BRANCH_WIDTH ** -0.5),
        "w_out": nrm(ks[15], (DEPTH, D_MODEL, D_MODEL), D_MODEL ** -0.5),
        "g_mlp": gain(ks[16], (DEPTH, D_MODEL)),
        "w_up": nrm(ks[17], (DEPTH, D_MODEL, D_FF), D_MODEL ** -0.5),
        "w_down": nrm(ks[18], (DEPTH, D_FF, D_MODEL), D_FF ** -0.5),
    }


def reference(x, positions, rel_bias, g_final, g_mix, w_in, swa_sinks, g_q_norm, w_q_up,
              g_kv_norm, w_kv_up, w_dw, b_dw, g_conv_ln, b_conv_ln, w_branch, w_out,
              g_mlp, w_up, w_down):
    b, s, _ = x.shape
    swa_bias, swa_mask = swa_bias_and_mask(positions, rel_bias)
    for l in range(DEPTH):
        h = rms_norm(x, g_mix[l])
        q_a, k_a, v_a, u_b, cq_c, ckv_c, kpe_c, gate_logits = jnp.split(h @ w_in[l], IN_SPLIT_POINTS, axis=-1)
        y_a = swa_attention(q_a, k_a, v_a, swa_sinks[l], swa_bias, swa_mask)
        y_b = conformer_conv(u_b, w_dw[l], b_dw[l], g_conv_ln[l], b_conv_ln[l])
        y_c = mla_attention(cq_c, ckv_c, kpe_c, positions, g_q_norm[l], w_q_up[l],
                            g_kv_norm[l], w_kv_up[l])
        branches = jnp.einsum('bsnc,ncd->bsnd', jnp.stack([y_a, y_b, y_c], axis=2), w_branch[l])
        gates = jax.nn.sigmoid(gate_logits.reshape(b, s, N_BRANCHES, D_MODEL))
        x = x + jnp.einsum('bsnd,bsnd->bsd', gates, branches) @ w_out[l]
        h = rms_norm(x, g_mlp[l])
        x = x + jnp.square(jax.nn.relu(h @ w_up[l])) @ w_down[l]
    return rms_norm(x, g_final)
```

```python
import contextlib
import math
import numpy as np
import concourse.bass as bass
import concourse.mybir as mybir
from concourse.bass_utils import run_bass_kernel_spmd

F32 = mybir.dt.float32
BF16 = mybir.dt.bfloat16
I32 = mybir.dt.int32
ALU = mybir.AluOpType
AF = mybir.ActivationFunctionType

ENGS = ("pe", "act", "dve", "pool", "sp")
N_DSEM = {"sp": 16, "pool": 16}


class Tk:
    __slots__ = ("ap", "w", "r", "name", "psum")

    def __init__(self, ap, name="", psum=False):
        self.ap = ap
        self.w = None
        self.r = []
        self.name = name
        self.psum = psum

    def __getitem__(self, k):
        return self.ap[k]


class _Rec:
    def __init__(self):
        self.call = None

    def __getattr__(self, name):
        def f(*a, **k):
            self.call = (name, a, k)
            return None
        return f


class Op:
    __slots__ = ("eng", "fn", "waits", "idx", "sig", "cnt", "dma", "dsem", "dval")


class Prog:
    def __init__(self, nc):
        self.nc = nc
        self.ops = {e: [] for e in ENGS}
        self.seen = {e: {} for e in ENGS}
        self.stack = contextlib.ExitStack()
        self.esem = {}
        self.dsems = {}
        self.dcount = {}
        self.drr = {e: 0 for e in N_DSEM}
        for e in ENGS:
            self.esem[e] = self.stack.enter_context(nc.semaphore("es_" + e))
        for e, n in N_DSEM.items():
            for i in range(n):
                self.dsems[(e, i)] = self.stack.enter_context(nc.semaphore("ds_%s%d" % (e, i)))
                self.dcount[(e, i)] = 0

    def sb(self, name, shape, dt):
        return self.stack.enter_context(self.nc.sbuf_tensor("s_" + name, list(shape), dt))

    def ps(self, name, shape, dt=F32):
        return self.stack.enter_context(self.nc.psum_tensor("p_" + name, list(shape), dt))

    def tile(self, name, shape, dt):
        return Tk(self.sb(name, shape, dt), name)

    def _need(self, op, dep):
        if dep is None or dep is op:
            return
        eng = op.eng
        if dep.dma:
            key = ("d",) + dep.dsem
            if self.seen[eng].get(key, 0) >= dep.dval:
                return
            self.seen[eng][key] = dep.dval
            op.waits.append(("d", dep))
            return
        if dep.eng == eng and not op.dma and eng == "pe":
            return
        key = dep.eng
        if self.seen[eng].get(key, -1) >= dep.idx:
            return
        self.seen[eng][key] = dep.idx
        dep.sig = True
        op.waits.append(("c", dep))

    def op(self, eng, fn, r=(), w=(), dma=False):
        o = Op()
        o.eng = eng
        rec = _Rec()
        fn(rec)
        o.fn = rec.call
        o.waits = []
        o.idx = len(self.ops[eng])
        o.sig = False
        o.cnt = 0
        o.dma = dma
        o.dsem = None
        o.dval = 0
        if dma:
            k = (eng, self.drr[eng] % N_DSEM[eng])
            self.drr[eng] += 1
            prev = self.dcount[k]
            self.dcount[k] = prev + 16
            o.dsem = k
            o.dval = prev + 16
            if prev > 0:
                key = ("d",) + k
                if self.seen[eng].get(key, 0) < prev:
                    self.seen[eng][key] = prev
                    o.waits.append(("dv", (k, prev)))
        for t in r:
            self._need(o, t.w)
            if t.psum:
                for rd in t.r:
                    if rd.eng != eng:
                        self._need(o, rd)
        for t in w:
            self._need(o, t.w)
            for rd in t.r:
                self._need(o, rd)
        for t in w:
            t.w = o
            t.r = []
        for t in r:
            if t in w:
                continue
            if not dma:
                t.r = [x for x in t.r if x.dma or x.eng != eng]
            t.r.append(o)
        self.ops[eng].append(o)
        return o

    def mm(self, out_ap, lhsT, rhs, start, stop, r=(), w=()):
        return self.op("pe", lambda e: e.matmul(out_ap, lhsT, rhs, start=start, stop=stop), r=r, w=w)

    def dma(self, eng, out_ap, in_ap, r=(), w=()):
        return self.op(eng, lambda e: e.dma_start(out=out_ap, in_=in_ap), r=r, w=w, dma=True)

    def finish(self, out_ops):
        o = Op()
        o.eng = "sp"; o.fn = None; o.waits = []; o.idx = len(self.ops["sp"]); o.sig = False
        o.cnt = 0; o.dma = False; o.dsem = None; o.dval = 0
        for d in out_ops:
            o.waits.append(("d", d))
        self.ops["sp"].append(o)

    def emit(self):
        nc = self.nc
        for eng in ENGS:
            c = 0
            for o in self.ops[eng]:
                if o.sig and not o.dma:
                    c += 1
                    o.cnt = c

        def run(engname):
            def f(e):
                for o in self.ops[engname]:
                    for kind, dep in o.waits:
                        if kind == "c":
                            e.wait_ge(self.esem[dep.eng], dep.cnt)
                        elif kind == "d":
                            e.wait_ge(self.dsems[dep.dsem], dep.dval)
                        else:
                            e.wait_ge(self.dsems[dep[0]], dep[1])
                    if o.fn is None:
                        continue
                    name, a, k = o.fn
                    ins = getattr(e, name)(*a, **k)
                    if o.dma:
                        ins.then_inc(self.dsems[o.dsem], 16)
                    elif o.sig:
                        ins.then_inc(self.esem[engname], 1)
            return f

        with nc.Block() as block:
            block.sync(run("sp"))
            block.tensor(run("pe"))
            block.scalar(run("act"))
            block.vector(run("dve"))
            block.gpsimd(run("pool"))


class _Stop(Exception):
    pass


class Ring:
    def __init__(self, tiles):
        self.t = tiles
        self.i = 0

    def next(self):
        t = self.t[self.i % len(self.t)]
        self.i += 1
        return t


D = 1024
G = 512
EPS = 1e-6
NH = 8
CONVW = 31
WSLOT = 2048
NWS = 4
V_GMIX, V_GMLP, V_GQ, V_GKV, V_WDW, V_BDW, V_GLN, V_BLN, V_SINK = 0, 8, 16, 18, 19, 143, 147, 151, 155
V_PER_LAYER = 163


def t5_thresholds():
    n = np.arange(0, 128)
    nf = np.maximum(n, 16).astype(np.float32)
    large = 16 + (np.log(nf / np.float32(16)) / np.float32(math.log(128 / 16)) * np.float32(16)).astype(np.int32)
    bucket = np.where(n < 16, n, np.minimum(large, 31))
    thr = []
    for j in range(1, 32):
        idx = np.nonzero(bucket >= j)[0]
        thr.append(int(idx[0]) if len(idx) else 1 << 20)
    return thr


def unit_list(nl):
    u = []
    for i in range(9):
        u.append(("a%d" % i, 8 * (256 if i < 8 else 192)))
    u.append(("v", 8 * 256))
    u.append(("q0", 2 * 1024))
    u.append(("q1", 2 * 1024))
    u.append(("kv", 1536))
    for dc in range(8):
        u.append(("m%da" % dc, 8 * 256))
        u.append(("m%db" % dc, 8 * 128 + 2 * 512))
        u.append(("m%dc" % dc, 512))
    for i in range(4):
        u.append(("o%d" % i, 8 * 256))
    for i in range(16):
        u.append(("u%d" % i, 8 * 256))
    for i in range(16):
        u.append(("d%d" % i, 16 * 128))
    return u


def build(S, NL, dbg=None):
    NG = S // G
    NV = NL * V_PER_LAYER + 8 + 2
    V_GFIN = NL * V_PER_LAYER
    V_ROPE = V_GFIN + 8
    units = unit_list(NL)
    upl = sum(n for _, n in units)
    nc = bass.Bass("TRN2", target_bir_lowering=False)
    xT_d = nc.dram_tensor("xT", [D, S], F32, kind="ExternalInput").ap()
    pos_d = nc.dram_tensor("pos", [1, S], I32, kind="ExternalInput").ap()
    relb_d = nc.dram_tensor("relb", [1, 256], F32, kind="ExternalInput").ap()
    vecs_d = nc.dram_tensor("vecs", [128, NV], F32, kind="ExternalInput").ap()
    w_d = nc.dram_tensor("wts", [NL, 128, upl], F32, kind="ExternalInput").ap()
    out_d = nc.dram_tensor("outT", [D, S], F32, kind="ExternalOutput").ap()

    p = Prog(nc)
    Xt = p.sb("X", [128, 8, G], F32)
    X = [Tk(Xt[:, c, :], "X%d" % c) for c in range(8)]
    vecs = p.tile("vecs", [128, NV], F32)
    biasT = p.tile("biasT", [128, 2, 8, 128], F32)
    onesf = p.tile("onesf", [128, 128], F32)
    onesb = p.tile("onesb", [128, 128], BF16)
    cmask = p.tile("cmask", [128, 128], BF16)
    ckvn = [p.tile("ckvn%d" % l, [128, S], BF16) for l in range(NL)]
    kper = [p.tile("kper%d" % l, [128, S], BF16) for l in range(NL)]
    kTs = [p.tile("kTs%d" % l, [128, 128 + G], BF16) for l in range(NL)]
    vxs = [p.tile("vxs%d" % l, [128, 5, 256], BF16) for l in range(NL)]
    U = p.tile("U", [128, 4, 32 + G], F32)
    halo = [p.tile("halo%d" % l, [128, 4, 32], F32) for l in range(NL)]
    CVt = p.sb("CV", [128, 4, G], F32)
    CV = [Tk(CVt[:, c, :], "CV%d" % c) for c in range(4)]
    NSL = 46
    WKt = p.sb("WK", [128, NSL, G], BF16)
    SL = [Tk(WKt[:, i, :], "SL%d" % i) for i in range(NSL)]
    FTl = [p.tile("FT%d" % i, [128, G], F32) for i in range(6)]
    FT = Ring(FTl)
    cosT = p.tile("cosT", [128, G], F32)
    sinT = p.tile("sinT", [128, G], F32)
    WS = [p.tile("WS%d" % i, [128, WSLOT], BF16) for i in range(NWS)]
    PBl = [Tk(p.ps("PB%d" % i, [128, G]), "PB%d" % i, psum=True) for i in range(4)]
    PB = Ring(PBl)
    PAl = [Tk(p.ps("PA%d" % i, [128, G]), "PA%d" % i, psum=True) for i in range(4)]

    hT = SL[0:8]
    qT = SL[8:12]
    cqn = SL[12:14]
    yA = SL[14:18]
    yB = SL[18:22]
    yC = SL[22:26]
    mg = SL[26:34]
    PT = Ring(SL[34:38])
    QR = Ring(SL[38:40])
    KR = Ring(SL[40:43])
    VR = Ring(SL[43:46])
    h2 = SL[32:40]
    hid = SL[0:32]

    def vcol(i):
        return vecs[:, i:i + 1]

    evac_rr = [0]

    def evac(out_ap, in_ap, r, w, eng=None):
        if eng is None:
            eng = ("act", "dve")[evac_rr[0] % 2]
            evac_rr[0] += 1
        if eng == "act":
            p.op("act", lambda e: e.activation(out_ap, in_ap, AF.Copy), r=r, w=w)
        else:
            p.op(eng, lambda e: e.tensor_copy(out_ap, in_ap), r=r, w=w)

    sched = []
    for g in range(NG):
        for l in range(NL):
            for ui in range(len(units)):
                sched.append((l, ui))
    uoff = np.cumsum([0] + [n for _, n in units])
    ws_state = {"issued": 0, "next": 0}

    def ws_issue_upto(k):
        while ws_state["issued"] < min(k, len(sched)):
            i = ws_state["issued"]
            l, ui = sched[i]
            n = units[ui][1]
            slot = WS[i % NWS]
            p.dma("pool", slot[:, 0:n], w_d[l, :, int(uoff[ui]):int(uoff[ui]) + n], w=[slot])
            ws_state["issued"] += 1

    def ws_get(name):
        i = ws_state["next"]
        l, ui = sched[i]
        assert units[ui][0] == name, (units[ui][0], name)
        ws_issue_upto(i + 1)
        ws_state["next"] += 1
        if i + NWS - 1 < len(sched):
            pass
        return WS[i % NWS]

    def ws_after():
        ws_issue_upto(ws_state["next"] + NWS - 1)

    p.dma("sp", vecs[:, :], vecs_d, w=[vecs])
    p.op("pool", lambda e: e.memset(onesf[:, :], 1.0), w=[onesf])
    p.op("pool", lambda e: e.memset(onesb[:, :], 1.0), w=[onesb])
    ri = p.tile("ri", [128, 128], I32)
    rf = FT.next()
    p.op("pool", lambda e: e.iota(ri[:, :], [[1, 128]], base=0, channel_multiplier=-1), w=[ri])
    p.op("dve", lambda e: e.tensor_copy(rf[:, 0:128], ri[:, :]), r=[ri], w=[rf])
    p.op("dve", lambda e: e.tensor_scalar(cmask[:, :], rf[:, 0:128], 0.0, None, ALU.is_ge), r=[rf], w=[cmask])
    for l in range(NL):
        o = l * V_PER_LAYER + V_SINK
        p.op("act", lambda e, o=o: e.activation(vecs[:, o:o + 8], vecs[:, o:o + 8], AF.Exp), r=[vecs], w=[vecs])
    rb = p.tile("rb", [128, 256], F32)
    dd = p.tile("dd", [128, 248], F32)
    p.dma("sp", rb[:, :], relb_d.to_broadcast([128, 256]), w=[rb])
    p.op("dve", lambda e: e.tensor_tensor(dd[:, :], rb[:, 8:256], rb[:, 0:248], ALU.subtract), r=[rb], w=[dd])
    thr = t5_thresholds()
    tmpm = FT.next()
    tmpt = FT.next()
    for kt in range(2):
        rff = FT.next()
        p.op("pool", lambda e, kt=kt: e.iota(ri[:, :], [[1, 128]], base=128 * (1 - kt), channel_multiplier=-1), w=[ri])
        p.op("dve", lambda e, rff=rff: e.tensor_copy(rff[:, 0:128], ri[:, :]), r=[ri], w=[rff])
        p.op("dve", lambda e, rff=rff: e.tensor_scalar(tmpm[:, 0:128], rff[:, 0:128], 0.0, -30000.0, ALU.is_lt, ALU.mult), r=[rff], w=[tmpm])
        p.op("dve", lambda e, rff=rff: e.tensor_scalar(tmpm[:, 128:256], rff[:, 0:128], 128.0, -30000.0, ALU.is_ge, ALU.mult), r=[rff], w=[tmpm])
        p.op("dve", lambda e: e.tensor_tensor(tmpm[:, 0:128], tmpm[:, 0:128], tmpm[:, 128:256], ALU.add), r=[tmpm], w=[tmpm])
        for h in range(8):
            p.op("dve", lambda e, kt=kt, h=h: e.tensor_scalar(biasT[:, kt, h, :], tmpm[:, 0:128], rb[:, h:h + 1], None, ALU.add), r=[tmpm, rb], w=[biasT])
        for j in range(1, 32):
            if thr[j - 1] > 127:
                continue
            p.op("dve", lambda e, rff=rff, j=j: e.tensor_scalar(tmpt[:, 0:128], rff[:, 0:128], float(thr[j - 1]), None, ALU.is_ge), r=[rff], w=[tmpt])
            for h in range(8):
                p.op("dve", lambda e, kt=kt, h=h, j=j: e.scalar_tensor_tensor(
                    biasT[:, kt, h, :], tmpt[:, 0:128], dd[:, (j - 1) * 8 + h:(j - 1) * 8 + h + 1], biasT[:, kt, h, :],
                    ALU.mult, ALU.add), r=[tmpt, dd], w=[biasT])

    def rmsnorm_to(dst, gbase, ncols=G):
        ps = PB.next()
        for c in range(8):
            sq = FT.next()
            p.op("act", lambda e, sq=sq, c=c: e.activation(sq[:, :], X[c][:, :], AF.Square), r=[X[c]], w=[sq])
            p.mm(ps[:, :], onesf[:, :], sq[:, :], c == 0, c == 7, r=[onesf, sq], w=[ps])
        rstd = FT.next()
        p.op("act", lambda e: e.activation(rstd[:, :], ps[:, :], AF.Sqrt, bias=EPS, scale=1.0 / D), r=[ps], w=[rstd])
        p.op("dve", lambda e: e.reciprocal(rstd[:, :], rstd[:, :]), r=[rstd], w=[rstd])
        for c in range(8):
            p.op("dve", lambda e, c=c: e.scalar_tensor_tensor(dst[c][:, :], X[c][:, :], vcol(gbase + c), rstd[:, :],
                                                              ALU.mult, ALU.mult), r=[X[c], rstd, vecs], w=[dst[c]])

    out_ops = []

    def stage(n):
        if isinstance(dbg, (int, float)) and dbg <= n:
            raise _Stop()

    for g in range(NG):
      try:
        t0 = g * G
        for c in range(8):
            p.dma("sp", X[c][:, :], xT_d[c * 128:(c + 1) * 128, t0:t0 + G], w=[X[c]])
        stage(1)
        posi = FT.next()
        posf = FT.next()
        p.dma("sp", posi[:, :].bitcast(I32), pos_d[0:1, t0:t0 + G].to_broadcast([128, G]), w=[posi])
        p.op("dve", lambda e: e.tensor_copy(posf[:, :], posi[:, :].bitcast(I32)), r=[posi], w=[posf])
        for tab, phase in ((sinT, 0.0), (cosT, math.pi / 2)):
            ang = FT.next()
            kf = FT.next()
            p.op("dve", lambda e, ang=ang, phase=phase: e.tensor_scalar(ang[:, :], posf[:, :], vcol(V_ROPE), phase, ALU.mult, ALU.add),
                 r=[posf, vecs], w=[ang])
            p.op("dve", lambda e, ang=ang, kf=kf: e.tensor_scalar(kf[:, :].bitcast(I32), ang[:, :], 1.0 / (2 * math.pi), None, ALU.mult),
                 r=[ang], w=[kf])
            p.op("dve", lambda e, kf=kf: e.tensor_copy(tab[:, :], kf[:, :].bitcast(I32)), r=[kf], w=[tab])
            p.op("dve", lambda e, ang=ang: e.scalar_tensor_tensor(ang[:, :], tab[:, :], -2 * math.pi, ang[:, :], ALU.mult, ALU.add),
                 r=[tab, ang], w=[ang])
            p.op("dve", lambda e, ang=ang, kf=kf: e.tensor_scalar(kf[:, :], ang[:, :], math.pi, -2 * math.pi, ALU.is_gt, ALU.mult), r=[ang], w=[kf])
            p.op("dve", lambda e, ang=ang, kf=kf: e.tensor_tensor(ang[:, :], ang[:, :], kf[:, :], ALU.add), r=[ang, kf], w=[ang])
            p.op("dve", lambda e, ang=ang, kf=kf: e.tensor_scalar(kf[:, :], ang[:, :], -math.pi, 2 * math.pi, ALU.is_lt, ALU.mult), r=[ang], w=[kf])
            p.op("dve", lambda e, ang=ang, kf=kf: e.tensor_tensor(ang[:, :], ang[:, :], kf[:, :], ALU.add), r=[ang, kf], w=[ang])
            p.op("act", lambda e, ang=ang, tab=tab: e.activation(tab[:, :], ang[:, :], AF.Sin), r=[ang], w=[tab])
        p.op("dve", lambda e: e.tensor_scalar(sinT[:, :], sinT[:, :], vcol(V_ROPE + 1), None, ALU.mult), r=[sinT, vecs], w=[sinT])

        for l in range(NL):
            vb = l * V_PER_LAYER
            stage(2)
            rmsnorm_to(hT, vb + V_GMIX)
            stage(3)
            chunks = []
            for c in range(4):
                chunks.append(("q", c, 128))
            chunks.append(("k", 0, 128))
            for c in range(4):
                chunks.append(("ua", c, 128))
                chunks.append(("ug", c, 128))
            chunks.append(("cq", 0, 128))
            chunks.append(("cq", 1, 128))
            chunks.append(("ckv", 0, 128))
            chunks.append(("kp", 0, 96))
            chunks.append(("kp", 1, 96))
            col = 0
            cur_unit = -1
            slot = None
            a_tmp = None
            kp_t1 = None
            ssq_cq = None
            sub = {"q": 3.1, "k": 3.2, "ua": 3.3, "ug": 3.3, "cq": 3.5, "ckv": 3.6, "kp": 3.7}
            first = True
            for kind, idx, M in chunks:
                if not first:
                    stage(sub[kind])
                first = False
                ui = col // 256
                if ui != cur_unit:
                    if slot is not None:
                        ws_after()
                    slot = ws_get("a%d" % ui)
                    cur_unit = ui
                wcols = 256 if ui < 8 else 192
                sv = slot[:, 0:8 * wcols].rearrange("p (k m) -> p k m", k=8)
                co = col - ui * 256
                ps = PB.next()
                for kc in range(8):
                    p.mm(ps[0:M, :], sv[:, kc, co:co + M], hT[kc][:, :], kc == 0, kc == 7, r=[slot, hT[kc]], w=[ps])
                col += M
                if kind == "q":
                    evac(qT[idx][:, :], ps[:, :], [ps], [qT[idx]])
                elif kind == "k":
                    evac(kTs[l][:, 128:128 + G], ps[:, :], [ps], [kTs[l]])
                elif kind == "ua":
                    a_tmp = FT.next()
                    evac(a_tmp[:, :], ps[:, :], [ps], [a_tmp], eng="dve")
                elif kind == "ug":
                    sg = FT.next()
                    p.op("act", lambda e, sg=sg, ps=ps: e.activation(sg[:, :], ps[:, :], AF.Sigmoid), r=[ps], w=[sg])
                    p.op("pool", lambda e, sg=sg, a_tmp=a_tmp, idx=idx: e.tensor_tensor(U[:, idx, 32:32 + G], a_tmp[:, :], sg[:, :], ALU.mult),
                         r=[a_tmp, sg], w=[U])
                elif kind == "cq":
                    evac(CV[idx][:, :], ps[:, :], [ps], [CV[idx]], eng="dve")
                    sq = FT.next()
                    p.op("act", lambda e, sq=sq, ps=ps: e.activation(sq[:, :], ps[:, :], AF.Square), r=[ps], w=[sq])
                    if idx == 0:
                        ssq_cq = PB.next()
                    p.mm(ssq_cq[:, :], onesf[:, :], sq[:, :], idx == 0, idx == 1, r=[onesf, sq], w=[ssq_cq])
                    if idx == 1:
                        rs = FT.next()
                        p.op("act", lambda e, rs=rs, s=ssq_cq: e.activation(rs[:, :], s[:, :], AF.Sqrt, bias=EPS, scale=1.0 / 256), r=[ssq_cq], w=[rs])
                        p.op("dve", lambda e, rs=rs: e.reciprocal(rs[:, :], rs[:, :]), r=[rs], w=[rs])
                        for c2 in range(2):
                            p.op("dve", lambda e, rs=rs, c2=c2: e.scalar_tensor_tensor(cqn[c2][:, :], CV[c2][:, :], vcol(vb + V_GQ + c2), rs[:, :],
                                                                                      ALU.mult, ALU.mult), r=[CV[c2], rs, vecs], w=[cqn[c2]])
                elif kind == "ckv":
                    evac(CV[2][:, :], ps[:, :], [ps], [CV[2]], eng="dve")
                    sq = FT.next()
                    p.op("act", lambda e, sq=sq, ps=ps: e.activation(sq[:, :], ps[:, :], AF.Square), r=[ps], w=[sq])
                    s2 = PB.next()
                    p.mm(s2[:, :], onesf[:, :], sq[:, :], True, True, r=[onesf, sq], w=[s2])
                    rs = FT.next()
                    p.op("act", lambda e, rs=rs, s2=s2: e.activation(rs[:, :], s2[:, :], AF.Sqrt, bias=EPS, scale=1.0 / 128), r=[s2], w=[rs])
                    p.op("dve", lambda e, rs=rs: e.reciprocal(rs[:, :], rs[:, :]), r=[rs], w=[rs])
                    p.op("dve", lambda e, rs=rs: e.scalar_tensor_tensor(ckvn[l][:, t0:t0 + G], CV[2][:, :], vcol(vb + V_GKV), rs[:, :],
                                                                        ALU.mult, ALU.mult), r=[CV[2], rs, vecs], w=[ckvn[l]])
                elif kind == "kp":
                    if idx == 0:
                        kp_t1 = FT.next()
                        p.op("dve", lambda e, t=kp_t1, ps=ps: e.tensor_tensor(t[64:96, :], ps[64:96, :], cosT[64:96, :], ALU.mult),
                             r=[ps, cosT], w=[kp_t1])
                    else:
                        t2 = FT.next()
                        p.op("dve", lambda e, t=t2, ps=ps: e.tensor_tensor(t[64:96, :], ps[64:96, :], sinT[64:96, :], ALU.mult),
                             r=[ps, sinT], w=[t2])
                        p.op("dve", lambda e, t=t2, t1=kp_t1: e.tensor_tensor(kper[l][64:96, t0:t0 + G], t1[64:96, :], t[64:96, :], ALU.add),
                             r=[kp_t1, t2], w=[kper[l]])
            ws_after()
            stage(3.9)
            slot = ws_get("v")
            sv = slot[:, 0:2048].rearrange("p (k m) -> p k m", k=8)
            for t in range(4):
                ps = PB.next()
                for kc in range(8):
                    p.mm(ps[:, 0:256], hT[kc][:, t * 128:(t + 1) * 128], sv[:, kc, :], kc == 0, kc == 7, r=[slot, hT[kc]], w=[ps])
                evac(vxs[l][:, 1 + t, :], ps[:, 0:256], [ps], [vxs[l]])
            ws_after()

            stage(4)
            qv = WKt[:, 8:12, :]
            yAv = WKt[:, 14:18, :]
            for b in range(4):
                gb = g * 4 + b
                for gq in range(2):
                    R = slice(gq * 64, gq * 64 + 64)
                    kts = [0, 1] if gb > 0 else [1]
                    pts = []
                    for kt in kts:
                        ps = PB.next()
                        kc0 = b * 128 + kt * 128
                        p.mm(ps[:, :], kTs[l][R, kc0:kc0 + 128], qv[R, :, b * 128:(b + 1) * 128], True, True,
                             r=[kTs[l]] + qT, w=[ps])
                        sc = FT.next()
                        bv = biasT[:, kt, gq * 4:(gq + 1) * 4, :]
                        p.op("dve", lambda e, sc=sc, ps=ps, bv=bv: e.scalar_tensor_tensor(
                            sc[:, :].rearrange("p (h q) -> p h q", h=4), ps[:, :].rearrange("p (h q) -> p h q", h=4), 0.125, bv,
                            ALU.mult, ALU.add), r=[ps, biasT], w=[sc])
                        pt = PT.next()
                        p.op("act", lambda e, pt=pt, sc=sc: e.activation(pt[:, :], sc[:, :], AF.Exp), r=[sc], w=[pt])
                        pts.append((kt, pt))
                    pso = PB.next()
                    psd = PB.next()
                    for i, (kt, pt) in enumerate(pts):
                        p.mm(pso[:, :], vxs[l][:, b + kt, gq * 128:(gq + 1) * 128], pt[:, :], i == 0, i == len(pts) - 1,
                             r=[vxs[l], pt], w=[pso])
                    for i, (kt, pt) in enumerate(pts):
                        p.mm(psd[:, :], onesb[:, :], pt[:, :], i == 0, i == len(pts) - 1, r=[onesb, pt], w=[psd])
                    den = FT.next()
                    for j in range(4):
                        sc_i = vb + V_SINK + gq * 4 + j
                        p.op("dve", lambda e, den=den, psd=psd, j=j, sc_i=sc_i, R=R: e.tensor_scalar(
                            den[R, j * 128:(j + 1) * 128], psd[R, j * 128:(j + 1) * 128], vecs[R, sc_i:sc_i + 1], None, ALU.add),
                            r=[psd, vecs], w=[den])
                    p.op("dve", lambda e, den=den, R=R: e.reciprocal(den[R, :], den[R, :]), r=[den], w=[den])
                    p.op("dve", lambda e, den=den, pso=pso, R=R, b=b: e.tensor_tensor(
                        yAv[R, :, b * 128:(b + 1) * 128], pso[R, :].rearrange("p (h q) -> p h q", h=4),
                        den[R, :].rearrange("p (h q) -> p h q", h=4), ALU.mult), r=[pso, den], w=yA)
            p.op("pool", lambda e, l=l: e.tensor_copy(kTs[l][:, 0:128], kTs[l][:, G:G + 128]), r=[kTs[l]], w=[kTs[l]])
            p.op("pool", lambda e, l=l: e.tensor_copy(vxs[l][:, 0, :], vxs[l][:, 4, :]), r=[vxs[l]], w=[vxs[l]])

            stage(5)
            if g == 0:
                p.op("pool", lambda e: e.memset(U[:, :, 0:32], 0.0), w=[U])
            else:
                p.op("pool", lambda e, l=l: e.tensor_copy(U[:, :, 0:32], halo[l][:, :, :]), r=[halo[l]], w=[U])
            wd0 = vb + V_WDW
            pss = PAl[0]
            psq = PAl[1]
            conv_ops = []
            for c in range(4):
                acc = CV[c]
                conv_ops.append((lambda c=c, acc=acc: p.op("dve", lambda e: e.tensor_scalar(
                    acc[:, :], U[:, c, 2:2 + G], vcol(wd0 + c * 31), vcol(vb + V_BDW + c), ALU.mult, ALU.add), r=[U, vecs], w=[acc])))
                for j in range(1, CONVW):
                    conv_ops.append((lambda c=c, acc=acc, j=j: p.op("dve", lambda e: e.scalar_tensor_tensor(
                        acc[:, :], U[:, c, 2 + j:2 + j + G], vcol(wd0 + c * 31 + j), acc[:, :], ALU.mult, ALU.add), r=[U, vecs, acc], w=[acc])))
            conv_ops.append(lambda l=l: p.op("pool", lambda e: e.tensor_copy(halo[l][:, :, :], U[:, :, G:G + 32]), r=[U], w=[halo[l]]))

            def conv_finish():
                for c in range(4):
                    acc = CV[c]
                    sq = FT.next()
                    p.op("act", lambda e: e.activation(sq[:, :], acc[:, :], AF.Square), r=[acc], w=[sq])
                    p.mm(pss[:, :], onesf[:, :], acc[:, :], c == 0, c == 3, r=[onesf, acc], w=[pss])
                    p.mm(psq[:, :], onesf[:, :], sq[:, :], c == 0, c == 3, r=[onesf, sq], w=[psq])
            def conv_ln():
                mean = FT.next()
                var = FT.next()
                p.op("act", lambda e, mean=mean: e.activation(mean[:, :], pss[:, :], AF.Copy, scale=1.0 / 512), r=[pss], w=[mean])
                p.op("dve", lambda e, mean=mean, var=var: e.tensor_tensor(var[:, :], mean[:, :], mean[:, :], ALU.mult), r=[mean], w=[var])
                p.op("dve", lambda e, var=var: e.scalar_tensor_tensor(var[:, :], psq[:, :], 1.0 / 512, var[:, :], ALU.mult, ALU.subtract),
                     r=[psq, var], w=[var])
                p.op("act", lambda e, var=var: e.activation(var[:, :], var[:, :], AF.Sqrt, bias=EPS), r=[var], w=[var])
                p.op("dve", lambda e, var=var: e.reciprocal(var[:, :], var[:, :]), r=[var], w=[var])
                for c in range(4):
                    acc = CV[c]
                    p.op("dve", lambda e, acc=acc, mean=mean: e.tensor_tensor(acc[:, :], acc[:, :], mean[:, :], ALU.subtract), r=[acc, mean], w=[acc])
                    p.op("pool", lambda e, acc=acc, var=var: e.tensor_tensor(acc[:, :], acc[:, :], var[:, :], ALU.mult), r=[acc, var], w=[acc])
                    p.op("act", lambda e, acc=acc, c=c: e.activation(yB[c][:, :], acc[:, :], AF.Silu, bias=vcol(vb + V_BLN + c), scale=vcol(vb + V_GLN + c)),
                         r=[acc, vecs], w=[yB[c]])


            stage(6)
            slq = [ws_get("q0"), None]
            slq[1] = ws_get("q1")
            slkv = ws_get("kv")
            per_head = (len(conv_ops) + NH - 1) // NH
            for h in range(NH):
                for cf in conv_ops[h * per_head:(h + 1) * per_head]:
                    cf()
                sq_slot = slq[h // 4]
                qv2 = sq_slot[:, 0:2048].rearrange("p (k m) -> p k m", k=2)
                hh = h % 4
                psm = PB.next()
                psp = PB.next()
                for kc in range(2):
                    p.mm(psm[0:96, :], qv2[:, kc, hh * 256:hh * 256 + 96], cqn[kc][:, :], kc == 0, kc == 1, r=[sq_slot, cqn[kc]], w=[psm])
                for kc in range(2):
                    p.mm(psp[0:96, :], qv2[:, kc, hh * 256 + 128:hh * 256 + 224], cqn[kc][:, :], kc == 0, kc == 1, r=[sq_slot, cqn[kc]], w=[psp])
                Qh = QR.next()
                p.op("act", lambda e, Qh=Qh, psm=psm: e.activation(Qh[0:64, :], psm[0:64, :], AF.Copy), r=[psm], w=[Qh])
                t1 = FT.next()
                t2 = FT.next()
                p.op("dve", lambda e, t1=t1, psm=psm: e.tensor_tensor(t1[64:96, :], psm[64:96, :], cosT[64:96, :], ALU.mult), r=[psm, cosT], w=[t1])
                p.op("dve", lambda e, t2=t2, psp=psp: e.tensor_tensor(t2[64:96, :], psp[64:96, :], sinT[64:96, :], ALU.mult), r=[psp, sinT], w=[t2])
                p.op("pool", lambda e, t1=t1, t2=t2, Qh=Qh: e.tensor_tensor(Qh[64:96, :], t1[64:96, :], t2[64:96, :], ALU.add), r=[t1, t2], w=[Qh])
                PO = PAl[2 + 0] if h % 2 == 0 else PAl[0]
                PD = PAl[2 + 1] if h % 2 == 0 else PAl[1]
                nkt = 4 * (g + 1)
                kv_state = {}

                def prep(j):
                    psk = PB.next()
                    p.mm(psk[0:64, :], slkv[:, h * 192:h * 192 + 64], ckvn[l][:, j * G:(j + 1) * G], True, True, r=[slkv, ckvn[l]], w=[psk])
                    KT = KR.next()
                    evac(KT[0:64, :], psk[0:64, :], [psk], [KT], eng="act")
                    p.op("pool", lambda e: e.tensor_copy(KT[64:96, :], kper[l][64:96, j * G:(j + 1) * G]), r=[kper[l]], w=[KT])
                    psv = PB.next()
                    for t in range(4):
                        tk0 = j * G + t * 128
                        p.mm(psv[:, t * 128:(t + 1) * 128], ckvn[l][:, tk0:tk0 + 128], slkv[:, h * 192 + 64:h * 192 + 192], True, True,
                             r=[slkv, ckvn[l]], w=[psv])
                    Vh = VR.next()
                    evac(Vh[:, :], psv[:, :], [psv], [Vh], eng="dve")
                    kv_state[j] = (KT, Vh)

                def score(kt):
                    j, t = divmod(kt, 4)
                    if j not in kv_state:
                        prep(j)
                    KT, Vh = kv_state[j]
                    q0 = 0 if j < g else t * 128
                    pss2 = PB.next()
                    p.mm(pss2[:, q0:G], KT[0:96, t * 128:(t + 1) * 128], Qh[0:96, q0:G], True, True, r=[KT, Qh], w=[pss2])
                    return pss2

                nxt = score(0)
                for kt in range(nkt):
                    j, t = divmod(kt, 4)
                    q0 = 0 if j < g else t * 128
                    pss2 = nxt
                    if kt + 1 < nkt:
                        nxt = score(kt + 1)
                    KT, Vh = kv_state[j]
                    pt = PT.next()
                    p.op("act", lambda e: e.activation(pt[:, q0:G], pss2[:, q0:G], AF.Exp, scale=96 ** -0.5), r=[pss2], w=[pt])
                    if j == g:
                        p.op("pool", lambda e: e.tensor_tensor(pt[:, q0:q0 + 128], pt[:, q0:q0 + 128], cmask[:, :], ALU.mult),
                             r=[pt, cmask], w=[pt])
                    p.mm(PO[:, q0:G], Vh[:, t * 128:(t + 1) * 128], pt[:, q0:G], kt == 0, kt == nkt - 1, r=[Vh, pt], w=[PO])
                    p.mm(PD[:, q0:G], onesb[:, :], pt[:, q0:G], kt == 0, kt == nkt - 1, r=[onesb, pt], w=[PD])
                Rh = slice((h % 2) * 64, (h % 2) * 64 + 64)
                rden = FT.next()
                p.op("dve", lambda e, rden=rden, PD=PD, Rh=Rh: e.reciprocal(rden[Rh, :], PD[Rh, :]), r=[PD], w=[rden])
                p.op("dve", lambda e, rden=rden, PO=PO, Rh=Rh, h=h: e.tensor_tensor(yC[h // 2][Rh, :], PO[Rh, :], rden[Rh, :], ALU.mult),
                     r=[PO, rden], w=[yC[h // 2]])

            conv_finish()
            conv_ln()
            ws_after()
            stage(7)
            ys = [yA, yB, yC]
            for dc in range(8):
                sa = ws_get("m%da" % dc)
                sav = sa[:, 0:2048].rearrange("p (k m) -> p k m", k=8)
                sbb = ws_get("m%db" % dc)
                sbg = sbb[:, 0:1024].rearrange("p (k m) -> p k m", k=8)
                sbw = sbb[:, 1024:2048].rearrange("p (n k m) -> p n k m", n=2, k=4)
                scc = ws_get("m%dc" % dc)
                scw = scc[:, 0:512].rearrange("p (k m) -> p k m", k=4)
                acc = FT.next()
                for n in range(3):
                    psg = PB.next()
                    for kc in range(8):
                        lw = sav[:, kc, n * 128:(n + 1) * 128] if n < 2 else sbg[:, kc, :]
                        p.mm(psg[:, :], lw, hT[kc][:, :], kc == 0, kc == 7, r=[sa if n < 2 else sbb, hT[kc]], w=[psg])
                    psb = PB.next()
                    for kc in range(4):
                        lw = sbw[:, n, kc, :] if n < 2 else scw[:, kc, :]
                        p.mm(psb[:, :], lw, ys[n][kc][:, :], kc == 0, kc == 3, r=[sbb if n < 2 else scc, ys[n][kc]], w=[psb])
                    sg = FT.next()
                    p.op("act", lambda e, sg=sg, psg=psg: e.activation(sg[:, :], psg[:, :], AF.Sigmoid), r=[psg], w=[sg])
                    if n == 0:
                        p.op("dve", lambda e, acc=acc, sg=sg, psb=psb: e.tensor_tensor(acc[:, :], psb[:, :], sg[:, :], ALU.mult), r=[psb, sg], w=[acc])
                    else:
                        p.op("dve", lambda e, sg=sg, psb=psb: e.tensor_tensor(sg[:, :], psb[:, :], sg[:, :], ALU.mult), r=[psb, sg], w=[sg])
                        if n == 1:
                            p.op("pool", lambda e, acc=acc, sg=sg: e.tensor_tensor(acc[:, :], acc[:, :], sg[:, :], ALU.add), r=[acc, sg], w=[acc])
                        else:
                            p.op("pool", lambda e, acc=acc, sg=sg, dc=dc: e.tensor_tensor(mg[dc][:, :], acc[:, :], sg[:, :], ALU.add),
                                 r=[acc, sg], w=[mg[dc]])
                ws_after()
            if dbg == "mix":
                continue
            stage(8)
            for oc in range(8):
                if oc % 2 == 0:
                    so = ws_get("o%d" % (oc // 2))
                    sov = so[:, 0:2048].rearrange("p (k m) -> p k m", k=8)
                ps = PB.next()
                for kc in range(8):
                    p.mm(ps[:, :], sov[:, kc, (oc % 2) * 128:(oc % 2) * 128 + 128], mg[kc][:, :], kc == 0, kc == 7, r=[so, mg[kc]], w=[ps])
                p.op("dve", lambda e, ps=ps, oc=oc: e.tensor_tensor(X[oc][:, :], X[oc][:, :], ps[:, :], ALU.add), r=[X[oc], ps], w=[X[oc]])
                if oc % 2 == 1:
                    ws_after()
            stage(9)
            rmsnorm_to(h2, vb + V_GMLP)
            for hc in range(32):
                if hc % 2 == 0:
                    su = ws_get("u%d" % (hc // 2))
                    suv = su[:, 0:2048].rearrange("p (k m) -> p k m", k=8)
                ps = PB.next()
                for kc in range(8):
                    p.mm(ps[:, :], suv[:, kc, (hc % 2) * 128:(hc % 2) * 128 + 128], h2[kc][:, :], kc == 0, kc == 7, r=[su, h2[kc]], w=[ps])
                rl = FT.next()
                p.op("act", lambda e, rl=rl, ps=ps: e.activation(rl[:, :], ps[:, :], AF.Relu), r=[ps], w=[rl])
                eng = "pool" if hc % 2 == 0 else "dve"
                p.op(eng, lambda e, rl=rl, hc=hc: e.tensor_tensor(hid[hc][:, :], rl[:, :], rl[:, :], ALU.mult), r=[rl], w=[hid[hc]])
                if hc % 2 == 1:
                    ws_after()
            for oc in range(8):
                ps = PB.next()
                for half in range(2):
                    sd = ws_get("d%d" % (oc * 2 + half))
                    sdv = sd[:, 0:2048].rearrange("p (k m) -> p k m", k=16)
                    for k2 in range(16):
                        kc = half * 16 + k2
                        p.mm(ps[:, :], sdv[:, k2, :], hid[kc][:, :], kc == 0, kc == 31, r=[sd, hid[kc]], w=[ps])
                    ws_after()
                p.op("dve", lambda e, ps=ps, oc=oc: e.tensor_tensor(X[oc][:, :], X[oc][:, :], ps[:, :], ALU.add), r=[X[oc], ps], w=[X[oc]])
        if dbg is None:
            rmsnorm_to(X, V_GFIN)
        for c in range(8):
            out_ops.append(p.dma("sp", out_d[c * 128:(c + 1) * 128, t0:t0 + G], X[c][:, :], r=[X[c]]))
      except _Stop:
        for c in range(8):
            out_ops.append(p.dma("sp", out_d[c * 128:(c + 1) * 128, g * G:(g + 1) * G], X[c][:, :], r=[X[c]]))
    p.finish(out_ops)
    p.emit()
    return nc, p


def _unit(Wsel):
    K, M = Wsel.shape
    return np.ascontiguousarray(Wsel.reshape(K // 128, 128, M).transpose(1, 0, 2)).reshape(128, -1)


def prep_weights(NL, w_in, w_q_up, w_kv_up, w_branch, w_out, w_up, w_down):
    layers = []
    perm = np.concatenate([np.arange(16, 32), np.arange(0, 16)])
    for l in range(NL):
        Wi = w_in[l]
        cols = []
        for c in range(4):
            cols.append(Wi[:, c * 64:(c + 1) * 64])
            cols.append(Wi[:, (4 + c) * 64:(5 + c) * 64])
        cols.append(Wi[:, 512:640])
        for c in range(4):
            cols.append(Wi[:, 768 + c * 128:768 + (c + 1) * 128])
            cols.append(Wi[:, 1280 + c * 128:1280 + (c + 1) * 128])
        cols.append(Wi[:, 1792:2048])
        cols.append(Wi[:, 2048:2176])
        z64 = np.zeros((D, 64), np.float32)
        kpe = Wi[:, 2176:2208]
        cols += [z64, kpe, z64, kpe[:, perm]]
        fm = np.concatenate(cols, axis=1)
        assert fm.shape[1] == 2240
        parts = []
        for i in range(9):
            parts.append(_unit(fm[:, i * 256:min((i + 1) * 256, 2240)]))
        v0 = Wi[:, 640:704]
        v1 = Wi[:, 704:768]
        parts.append(_unit(np.concatenate([v0, v0, v1, v1], axis=1)))
        wq = w_q_up[l]
        for half in range(2):
            hc = []
            for h in range(half * 4, half * 4 + 4):
                blk = wq[:, h * 96:(h + 1) * 96]
                pe = blk[:, 64:96]
                z32 = np.zeros((256, 32), np.float32)
                z64q = np.zeros((256, 64), np.float32)
                hc += [blk, z32, z64q, pe[:, perm], z32]
            parts.append(_unit(np.concatenate(hc, axis=1)))
        wkv = w_kv_up[l]
        hc = []
        for h in range(8):
            blk = wkv[:, h * 128:(h + 1) * 128]
            hc += [blk[:, 0:64], blk[:, 64:128], blk[:, 64:128]]
        parts.append(_unit(np.concatenate(hc, axis=1)))
        gates = Wi[:, 2208:]
        wb = w_branch[l]
        for dc in range(8):
            g0 = gates[:, 0 * 1024 + dc * 128:0 * 1024 + (dc + 1) * 128]
            g1 = gates[:, 1 * 1024 + dc * 128:1 * 1024 + (dc + 1) * 128]
            g2 = gates[:, 2 * 1024 + dc * 128:2 * 1024 + (dc + 1) * 128]
            parts.append(_unit(np.concatenate([g0, g1], axis=1)))
            wbs = []
            for n in range(3):
                Wn = wb[n]
                if n == 0:
                    rows = []
                    for c in range(4):
                        rows += list(range(c * 64, (c + 1) * 64)) + list(range((4 + c) * 64, (5 + c) * 64))
                    Wn = Wn[rows]
                wbs.append(_unit(Wn[:, dc * 128:(dc + 1) * 128]))
            parts.append(np.concatenate([_unit(g2), wbs[0], wbs[1]], axis=1))
            parts.append(wbs[2])
        for i in range(4):
            parts.append(_unit(w_out[l][:, i * 256:(i + 1) * 256]))
        for i in range(16):
            parts.append(_unit(w_up[l][:, i * 256:(i + 1) * 256]))
        for oc in range(8):
            for half in range(2):
                parts.append(_unit(w_down[l][half * 2048:(half + 1) * 2048, oc * 128:(oc + 1) * 128]))
        layers.append(np.concatenate(parts, axis=1))
    return np.ascontiguousarray(np.stack(layers, 0))


def prep_vecs(NL, g_final, g_mix, g_q_norm, g_kv_norm, w_dw, b_dw, g_conv_ln, b_conv_ln, swa_sinks, g_mlp):
    NV = NL * V_PER_LAYER + 8 + 2
    v = np.zeros((128, NV), np.float32)
    for l in range(NL):
        b = l * V_PER_LAYER
        v[:, b + V_GMIX:b + V_GMIX + 8] = g_mix[l].reshape(8, 128).T
        v[:, b + V_GMLP:b + V_GMLP + 8] = g_mlp[l].reshape(8, 128).T
        v[:, b + V_GQ:b + V_GQ + 2] = g_q_norm[l].reshape(2, 128).T
        v[:, b + V_GKV] = g_kv_norm[l]
        wd = w_dw[l].reshape(CONVW, 4, 128)
        v[:, b + V_WDW:b + V_WDW + 124] = wd.transpose(2, 1, 0).reshape(128, 124)
        v[:, b + V_BDW:b + V_BDW + 4] = b_dw[l].reshape(4, 128).T
        v[:, b + V_GLN:b + V_GLN + 4] = g_conv_ln[l].reshape(4, 128).T
        v[:, b + V_BLN:b + V_BLN + 4] = b_conv_ln[l].reshape(4, 128).T
        v[:, b + V_SINK:b + V_SINK + 8] = swa_sinks[l][None, :]
    o = NL * V_PER_LAYER
    v[:, o:o + 8] = g_final.reshape(8, 128).T
    pp = np.arange(128)
    i = (pp - 64) % 16
    freq = np.exp(np.float32(-math.log(10000.0)) * i.astype(np.float32) / np.float32(16)).astype(np.float32)
    v[:, o + 8] = freq
    v[:, o + 9] = np.where(((pp - 64) % 32) < 16, -1.0, 1.0)
    return v


_CACHE = {}


def kernel(x, positions, rel_bias, g_final, g_mix, w_in, swa_sinks, g_q_norm, w_q_up, g_kv_norm, w_kv_up,
           w_dw, b_dw, g_conv_ln, b_conv_ln, w_branch, w_out, g_mlp, w_up, w_down):
    x = np.asarray(x, np.float32)
    B, S, _ = x.shape
    NL = int(np.asarray(w_in).shape[0])
    f = lambda a: np.asarray(a, np.float32)
    wts = prep_weights(NL, f(w_in), f(w_q_up), f(w_kv_up), f(w_branch), f(w_out), f(w_up), f(w_down))
    vecs = prep_vecs(NL, f(g_final), f(g_mix), f(g_q_norm), f(g_kv_norm), f(w_dw), f(b_dw), f(g_conv_ln),
                     f(b_conv_ln), f(swa_sinks), f(g_mlp))
    relb = f(rel_bias).reshape(1, 256)
    pos = np.asarray(positions, np.int32)
    key = (S, NL)
    if key not in _CACHE:
        _CACHE[key] = build(S, NL)[0]
    nc = _CACHE[key]
    in_maps = []
    for b in range(B):
        in_maps.append({"xT": np.ascontiguousarray(x[b].T), "pos": np.ascontiguousarray(pos[b:b + 1]),
                        "relb": relb, "vecs": vecs, "wts": wts})
    res = run_bass_kernel_spmd(nc, in_maps, core_ids=list(range(B)))
    out = np.stack([np.ascontiguousarray(r["outT"].T) for r in res.results], 0)
    return out.astype(np.float32)
```

```python
import contextlib
import math
import numpy as np
import concourse.bass as bass
import concourse.mybir as mybir
from concourse.bass_utils import run_bass_kernel_spmd

F32 = mybir.dt.float32
BF16 = mybir.dt.bfloat16
I32 = mybir.dt.int32
ALU = mybir.AluOpType
AF = mybir.ActivationFunctionType

ENGS = ("pe", "act", "dve", "pool", "sp")
N_DSEM = {"sp": 16, "pool": 16}


class Tk:
    __slots__ = ("ap", "w", "r", "name", "psum")

    def __init__(self, ap, name="", psum=False):
        self.ap = ap
        self.w = None
        self.r = []
        self.name = name
        self.psum = psum

    def __getitem__(self, k):
        return self.ap[k]


class _Rec:
    def __init__(self):
        self.call = None

    def __getattr__(self, name):
        def f(*a, **k):
            self.call = (name, a, k)
            return None
        return f


class Op:
    __slots__ = ("eng", "fn", "waits", "idx", "sig", "cnt", "dma", "dsem", "dval")


class Prog:
    def __init__(self, nc):
        self.nc = nc
        self.ops = {e: [] for e in ENGS}
        self.seen = {e: {} for e in ENGS}
        self.stack = contextlib.ExitStack()
        self.esem = {}
        self.dsems = {}
        self.dcount = {}
        self.drr = {e: 0 for e in N_DSEM}
        for e in ENGS:
            self.esem[e] = self.stack.enter_context(nc.semaphore("es_" + e))
        for e, n in N_DSEM.items():
            for i in range(n):
                self.dsems[(e, i)] = self.stack.enter_context(nc.semaphore("ds_%s%d" % (e, i)))
                self.dcount[(e, i)] = 0

    def sb(self, name, shape, dt):
        return self.stack.enter_context(self.nc.sbuf_tensor("s_" + name, list(shape), dt))

    def ps(self, name, shape, dt=F32):
        return self.stack.enter_context(self.nc.psum_tensor("p_" + name, list(shape), dt))

    def tile(self, name, shape, dt):
        return Tk(self.sb(name, shape, dt), name)

    def _need(self, op, dep):
        if dep is None or dep is op:
            return
        eng = op.eng
        if dep.dma:
            key = ("d",) + dep.dsem
            if self.seen[eng].get(key, 0) >= dep.dval:
                return
            self.seen[eng][key] = dep.dval
            op.waits.append(("d", dep))
            return
        if dep.eng == eng and not op.dma and eng == "pe":
            return
        key = dep.eng
        if self.seen[eng].get(key, -1) >= dep.idx:
            return
        self.seen[eng][key] = dep.idx
        dep.sig = True
        op.waits.append(("c", dep))

    def op(self, eng, fn, r=(), w=(), dma=False):
        o = Op()
        o.eng = eng
        rec = _Rec()
        fn(rec)
        o.fn = rec.call
        o.waits = []
        o.idx = len(self.ops[eng])
        o.sig = False
        o.cnt = 0
        o.dma = dma
        o.dsem = None
        o.dval = 0
        if dma:
            k = (eng, self.drr[eng] % N_DSEM[eng])
            self.drr[eng] += 1
            prev = self.dcount[k]
            self.dcount[k] = prev + 16
            o.dsem = k
            o.dval = prev + 16
            if prev > 0:
                key = ("d",) + k
                if self.seen[eng].get(key, 0) < prev:
                    self.seen[eng][key] = prev
                    o.waits.append(("dv", (k, prev)))
        for t in r:
            self._need(o, t.w)
            if t.psum:
                for rd in t.r:
                    if rd.eng != eng:
                        self._need(o, rd)
        for t in w:
            self._need(o, t.w)
            for rd in t.r:
                self._need(o, rd)
        for t in w:
            t.w = o
            t.r = []
        for t in r:
            if t in w:
                continue
            if not dma:
                t.r = [x for x in t.r if x.dma or x.eng != eng]
            t.r.append(o)
        self.ops[eng].append(o)
        return o

    def mm(self, out_ap, lhsT, rhs, start, stop, r=(), w=()):
        return self.op("pe", lambda e: e.matmul(out_ap, lhsT, rhs, start=start, stop=stop), r=r, w=w)

    def dma(self, eng, out_ap, in_ap, r=(), w=()):
        return self.op(eng, lambda e: e.dma_start(out=out_ap, in_=in_ap), r=r, w=w, dma=True)

    def finish(self, out_ops):
        o = Op()
        o.eng = "sp"; o.fn = None; o.waits = []; o.idx = len(self.ops["sp"]); o.sig = False
        o.cnt = 0; o.dma = False; o.dsem = None; o.dval = 0
        for d in out_ops:
            o.waits.append(("d", d))
        self.ops["sp"].append(o)

    def emit(self):
        nc = self.nc
        for eng in ENGS:
            c = 0
            for o in self.ops[eng]:
                if o.sig and not o.dma:
                    c += 1
                    o.cnt = c

        def run(engname):
            def f(e):
                for o in self.ops[engname]:
                    for kind, dep in o.waits:
                        if kind == "c":
                            e.wait_ge(self.esem[dep.eng], dep.cnt)
                        elif kind == "d":
                            e.wait_ge(self.dsems[dep.dsem], dep.dval)
                        else:
                            e.wait_ge(self.dsems[dep[0]], dep[1])
                    if o.fn is None:
                        continue
                    name, a, k = o.fn
                    ins = getattr(e, name)(*a, **k)
                    if o.dma:
                        ins.then_inc(self.dsems[o.dsem], 16)
                    elif o.sig:
                        ins.then_inc(self.esem[engname], 1)
            return f

        with nc.Block() as block:
            block.sync(run("sp"))
            block.tensor(run("pe"))
            block.scalar(run("act"))
            block.vector(run("dve"))
            block.gpsimd(run("pool"))


class _Stop(Exception):
    pass


class Ring:
    def __init__(self, tiles):
        self.t = tiles
        self.i = 0

    def next(self):
        t = self.t[self.i % len(self.t)]
        self.i += 1
        return t


D = 1024
G = 512
EPS = 1e-6
NH = 8
CONVW = 31
WSLOT = 2048
NWS = 4
V_GMIX, V_GMLP, V_GQ, V_GKV, V_WDW, V_BDW, V_GLN, V_BLN, V_SINK = 0, 8, 16, 18, 19, 143, 147, 151, 155
V_PER_LAYER = 163


def t5_thresholds():
    n = np.arange(0, 128)
    nf = np.maximum(n, 16).astype(np.float32)
    large = 16 + (np.log(nf / np.float32(16)) / np.float32(math.log(128 / 16)) * np.float32(16)).astype(np.int32)
    bucket = np.where(n < 16, n, np.minimum(large, 31))
    thr = []
    for j in range(1, 32):
        idx = np.nonzero(bucket >= j)[0]
        thr.append(int(idx[0]) if len(idx) else 1 << 20)
    return thr


def unit_list(nl):
    u = []
    for i in range(9):
        u.append(("a%d" % i, 8 * (256 if i < 8 else 192)))
    u.append(("v", 8 * 256))
    u.append(("q0", 2 * 1024))
    u.append(("q1", 2 * 1024))
    u.append(("kv", 1536))
    for dc in range(8):
        u.append(("m%da" % dc, 8 * 256))
        u.append(("m%db" % dc, 8 * 128 + 2 * 512))
        u.append(("m%dc" % dc, 512))
    for i in range(4):
        u.append(("o%d" % i, 8 * 256))
    for i in range(16):
        u.append(("u%d" % i, 8 * 256))
    for i in range(16):
        u.append(("d%d" % i, 16 * 128))
    return u


def build(S, NL, dbg=None):
    NG = S // G
    NV = NL * V_PER_LAYER + 8 + 2
    V_GFIN = NL * V_PER_LAYER
    V_ROPE = V_GFIN + 8
    units = unit_list(NL)
    upl = sum(n for _, n in units)
    nc = bass.Bass("TRN2", target_bir_lowering=False)
    xT_d = nc.dram_tensor("xT", [D, S], F32, kind="ExternalInput").ap()
    pos_d = nc.dram_tensor("pos", [1, S], I32, kind="ExternalInput").ap()
    relb_d = nc.dram_tensor("relb", [1, 256], F32, kind="ExternalInput").ap()
    vecs_d = nc.dram_tensor("vecs", [128, NV], F32, kind="ExternalInput").ap()
    w_d = nc.dram_tensor("wts", [NL, 128, upl], F32, kind="ExternalInput").ap()
    out_d = nc.dram_tensor("outT", [D, S], F32, kind="ExternalOutput").ap()

    p = Prog(nc)
    Xt = p.sb("X", [128, 8, G], F32)
    X = [Tk(Xt[:, c, :], "X%d" % c) for c in range(8)]
    vecs = p.tile("vecs", [128, NV], F32)
    biasT = p.tile("biasT", [128, 2, 8, 128], F32)
    onesf = p.tile("onesf", [128, 128], F32)
    onesb = p.tile("onesb", [128, 128], BF16)
    cmask = p.tile("cmask", [128, 128], BF16)
    ckvn = [p.tile("ckvn%d" % l, [128, S], BF16) for l in range(NL)]
    kper = [p.tile("kper%d" % l, [128, S], BF16) for l in range(NL)]
    kTs = [p.tile("kTs%d" % l, [128, 128 + G], BF16) for l in range(NL)]
    vxs = [p.tile("vxs%d" % l, [128, 5, 256], BF16) for l in range(NL)]
    U = p.tile("U", [128, 4, 32 + G], F32)
    halo = [p.tile("halo%d" % l, [128, 4, 32], F32) for l in range(NL)]
    CVt = p.sb("CV", [128, 4, G], F32)
    CV = [Tk(CVt[:, c, :], "CV%d" % c) for c in range(4)]
    NSL = 46
    WKt = p.sb("WK", [128, NSL, G], BF16)
    SL = [Tk(WKt[:, i, :], "SL%d" % i) for i in range(NSL)]
    FTl = [p.tile("FT%d" % i, [128, G], F32) for i in range(6)]
    FT = Ring(FTl)
    cosT = p.tile("cosT", [128, G], F32)
    sinT = p.tile("sinT", [128, G], F32)
    WS = [p.tile("WS%d" % i, [128, WSLOT], BF16) for i in range(NWS)]
    PBl = [Tk(p.ps("PB%d" % i, [128, G]), "PB%d" % i, psum=True) for i in range(4)]
    PB = Ring(PBl)
    PAl = [Tk(p.ps("PA%d" % i, [128, G]), "PA%d" % i, psum=True) for i in range(4)]

    hT = SL[0:8]
    qT = SL[8:12]
    cqn = SL[12:14]
    yA = SL[14:18]
    yB = SL[18:22]
    yC = SL[22:26]
    mg = SL[26:34]
    PT = Ring(SL[34:38])
    QR = Ring(SL[38:40])
    KR = Ring(SL[40:43])
    VR = Ring(SL[43:46])
    h2 = SL[32:40]
    hid = SL[0:32]

    def vcol(i):
        return vecs[:, i:i + 1]

    evac_rr = [0]

    def evac(out_ap, in_ap, r, w, eng=None):
        if eng is None:
            eng = ("act", "dve")[evac_rr[0] % 2]
            evac_rr[0] += 1
        if eng == "act":
            p.op("act", lambda e: e.activation(out_ap, in_ap, AF.Copy), r=r, w=w)
        else:
            p.op(eng, lambda e: e.tensor_copy(out_ap, in_ap), r=r, w=w)

    sched = []
    for g in range(NG):
        for l in range(NL):
            for ui in range(len(units)):
                sched.append((l, ui))
    uoff = np.cumsum([0] + [n for _, n in units])
    ws_state = {"issued": 0, "next": 0}

    def ws_issue_upto(k):
        while ws_state["issued"] < min(k, len(sched)):
            i = ws_state["issued"]
            l, ui = sched[i]
            n = units[ui][1]
            slot = WS[i % NWS]
            p.dma("pool", slot[:, 0:n], w_d[l, :, int(uoff[ui]):int(uoff[ui]) + n], w=[slot])
            ws_state["issued"] += 1

    def ws_get(name):
        i = ws_state["next"]
        l, ui = sched[i]
        assert units[ui][0] == name, (units[ui][0], name)
        ws_issue_upto(i + 1)
        ws_state["next"] += 1
        if i + NWS - 1 < len(sched):
            pass
        return WS[i % NWS]

    def ws_after():
        ws_issue_upto(ws_state["next"] + NWS - 1)

    p.dma("sp", vecs[:, :], vecs_d, w=[vecs])
    p.op("pool", lambda e: e.memset(onesf[:, :], 1.0), w=[onesf])
    p.op("pool", lambda e: e.memset(onesb[:, :], 1.0), w=[onesb])
    ri = p.tile("ri", [128, 128], I32)
    rf = FT.next()
    p.op("pool", lambda e: e.iota(ri[:, :], [[1, 128]], base=0, channel_multiplier=-1), w=[ri])
    p.op("dve", lambda e: e.tensor_copy(rf[:, 0:128], ri[:, :]), r=[ri], w=[rf])
    p.op("dve", lambda e: e.tensor_scalar(cmask[:, :], rf[:, 0:128], 0.0, None, ALU.is_ge), r=[rf], w=[cmask])
    for l in range(NL):
        o = l * V_PER_LAYER + V_SINK
        p.op("act", lambda e, o=o: e.activation(vecs[:, o:o + 8], vecs[:, o:o + 8], AF.Exp), r=[vecs], w=[vecs])
    rb = p.tile("rb", [128, 256], F32)
    dd = p.tile("dd", [128, 248], F32)
    p.dma("sp", rb[:, :], relb_d.to_broadcast([128, 256]), w=[rb])
    p.op("dve", lambda e: e.tensor_tensor(dd[:, :], rb[:, 8:256], rb[:, 0:248], ALU.subtract), r=[rb], w=[dd])
    thr = t5_thresholds()
    tmpm = FT.next()
    tmpt = FT.next()
    for kt in range(2):
        rff = FT.next()
        p.op("pool", lambda e, kt=kt: e.iota(ri[:, :], [[1, 128]], base=128 * (1 - kt), channel_multiplier=-1), w=[ri])
        p.op("dve", lambda e, rff=rff: e.tensor_copy(rff[:, 0:128], ri[:, :]), r=[ri], w=[rff])
        p.op("dve", lambda e, rff=rff: e.tensor_scalar(tmpm[:, 0:128], rff[:, 0:128], 0.0, -30000.0, ALU.is_lt, ALU.mult), r=[rff], w=[tmpm])
        p.op("dve", lambda e, rff=rff: e.tensor_scalar(tmpm[:, 128:256], rff[:, 0:128], 128.0, -30000.0, ALU.is_ge, ALU.mult), r=[rff], w=[tmpm])
        p.op("dve", lambda e: e.tensor_tensor(tmpm[:, 0:128], tmpm[:, 0:128], tmpm[:, 128:256], ALU.add), r=[tmpm], w=[tmpm])
        for h in range(8):
            p.op("dve", lambda e, kt=kt, h=h: e.tensor_scalar(biasT[:, kt, h, :], tmpm[:, 0:128], rb[:, h:h + 1], None, ALU.add), r=[tmpm, rb], w=[biasT])
        for j in range(1, 32):
            if thr[j - 1] > 127:
                continue
            p.op("dve", lambda e, rff=rff, j=j: e.tensor_scalar(tmpt[:, 0:128], rff[:, 0:128], float(thr[j - 1]), None, ALU.is_ge), r=[rff], w=[tmpt])
            for h in range(8):
                p.op("dve", lambda e, kt=kt, h=h, j=j: e.scalar_tensor_tensor(
                    biasT[:, kt, h, :], tmpt[:, 0:128], dd[:, (j - 1) * 8 + h:(j - 1) * 8 + h + 1], biasT[:, kt, h, :],
                    ALU.mult, ALU.add), r=[tmpt, dd], w=[biasT])

    def rmsnorm_to(dst, gbase, ncols=G):
        ps = PB.next()
        for c in range(8):
            sq = FT.next()
            p.op("act", lambda e, sq=sq, c=c: e.activation(sq[:, :], X[c][:, :], AF.Square), r=[X[c]], w=[sq])
            p.mm(ps[:, :], onesf[:, :], sq[:, :], c == 0, c == 7, r=[onesf, sq], w=[ps])
        rstd = FT.next()
        p.op("act", lambda e: e.activation(rstd[:, :], ps[:, :], AF.Sqrt, bias=EPS, scale=1.0 / D), r=[ps], w=[rstd])
        p.op("dve", lambda e: e.reciprocal(rstd[:, :], rstd[:, :]), r=[rstd], w=[rstd])
        for c in range(8):
            p.op("dve", lambda e, c=c: e.scalar_tensor_tensor(dst[c][:, :], X[c][:, :], vcol(gbase + c), rstd[:, :],
                                                              ALU.mult, ALU.mult), r=[X[c], rstd, vecs], w=[dst[c]])

    out_ops = []

    def stage(n):
        if isinstance(dbg, (int, float)) and dbg <= n:
            raise _Stop()

    for g in range(NG):
      try:
        t0 = g * G
        for c in range(8):
            p.dma("sp", X[c][:, :], xT_d[c * 128:(c + 1) * 128, t0:t0 + G], w=[X[c]])
        stage(1)
        posi = FT.next()
        posf = FT.next()
        p.dma("sp", posi[:, :].bitcast(I32), pos_d[0:1, t0:t0 + G].to_broadcast([128, G]), w=[posi])
        p.op("dve", lambda e: e.tensor_copy(posf[:, :], posi[:, :].bitcast(I32)), r=[posi], w=[posf])
        for tab, phase in ((sinT, 0.0), (cosT, math.pi / 2)):
            ang = FT.next()
            kf = FT.next()
            p.op("dve", lambda e, ang=ang, phase=phase: e.tensor_scalar(ang[:, :], posf[:, :], vcol(V_ROPE), phase, ALU.mult, ALU.add),
                 r=[posf, vecs], w=[ang])
            p.op("dve", lambda e, ang=ang, kf=kf: e.tensor_scalar(kf[:, :].bitcast(I32), ang[:, :], 1.0 / (2 * math.pi), None, ALU.mult),
                 r=[ang], w=[kf])
            p.op("dve", lambda e, kf=kf: e.tensor_copy(tab[:, :], kf[:, :].bitcast(I32)), r=[kf], w=[tab])
            p.op("dve", lambda e, ang=ang: e.scalar_tensor_tensor(ang[:, :], tab[:, :], -2 * math.pi, ang[:, :], ALU.mult, ALU.add),
                 r=[tab, ang], w=[ang])
            p.op("dve", lambda e, ang=ang, kf=kf: e.tensor_scalar(kf[:, :], ang[:, :], math.pi, -2 * math.pi, ALU.is_gt, ALU.mult), r=[ang], w=[kf])
            p.op("dve", lambda e, ang=ang, kf=kf: e.tensor_tensor(ang[:, :], ang[:, :], kf[:, :], ALU.add), r=[ang, kf], w=[ang])
            p.op("dve", lambda e, ang=ang, kf=kf: e.tensor_scalar(kf[:, :], ang[:, :], -math.pi, 2 * math.pi, ALU.is_lt, ALU.mult), r=[ang], w=[kf])
            p.op("dve", lambda e, ang=ang, kf=kf: e.tensor_tensor(ang[:, :], ang[:, :], kf[:, :], ALU.add), r=[ang, kf], w=[ang])
            p.op("act", lambda e, ang=ang, tab=tab: e.activation(tab[:, :], ang[:, :], AF.Sin), r=[ang], w=[tab])
        p.op("dve", lambda e: e.tensor_scalar(sinT[:, :], sinT[:, :], vcol(V_ROPE + 1), None, ALU.mult), r=[sinT, vecs], w=[sinT])

        for l in range(NL):
            vb = l * V_PER_LAYER
            stage(2)
            rmsnorm_to(hT, vb + V_GMIX)
            stage(3)
            chunks = []
            for c in range(4):
                chunks.append(("q", c, 128))
            chunks.append(("k", 0, 128))
            for c in range(4):
                chunks.append(("ua", c, 128))
                chunks.append(("ug", c, 128))
            chunks.append(("cq", 0, 128))
            chunks.append(("cq", 1, 128))
            chunks.append(("ckv", 0, 128))
            chunks.append(("kp", 0, 96))
            chunks.append(("kp", 1, 96))
            col = 0
            cur_unit = -1
            slot = None
            a_tmp = None
            kp_t1 = None
            ssq_cq = None
            sub = {"q": 3.1, "k": 3.2, "ua": 3.3, "ug": 3.3, "cq": 3.5, "ckv": 3.6, "kp": 3.7}
            first = True
            for kind, idx, M in chunks:
                if not first:
                    stage(sub[kind])
                first = False
                ui = col // 256
                if ui != cur_unit:
                    if slot is not None:
                        ws_after()
                    slot = ws_get("a%d" % ui)
                    cur_unit = ui
                wcols = 256 if ui < 8 else 192
                sv = slot[:, 0:8 * wcols].rearrange("p (k m) -> p k m", k=8)
                co = col - ui * 256
                ps = PB.next()
                for kc in range(8):
                    p.mm(ps[0:M, :], sv[:, kc, co:co + M], hT[kc][:, :], kc == 0, kc == 7, r=[slot, hT[kc]], w=[ps])
                col += M
                if kind == "q":
                    evac(qT[idx][:, :], ps[:, :], [ps], [qT[idx]])
                elif kind == "k":
                    evac(kTs[l][:, 128:128 + G], ps[:, :], [ps], [kTs[l]])
                elif kind == "ua":
                    a_tmp = FT.next()
                    evac(a_tmp[:, :], ps[:, :], [ps], [a_tmp], eng="dve")
                elif kind == "ug":
                    sg = FT.next()
                    p.op("act", lambda e, sg=sg, ps=ps: e.activation(sg[:, :], ps[:, :], AF.Sigmoid), r=[ps], w=[sg])
                    p.op("pool", lambda e, sg=sg, a_tmp=a_tmp, idx=idx: e.tensor_tensor(U[:, idx, 32:32 + G], a_tmp[:, :], sg[:, :], ALU.mult),
                         r=[a_tmp, sg], w=[U])
                elif kind == "cq":
                    evac(CV[idx][:, :], ps[:, :], [ps], [CV[idx]], eng="dve")
                    sq = FT.next()
                    p.op("act", lambda e, sq=sq, ps=ps: e.activation(sq[:, :], ps[:, :], AF.Square), r=[ps], w=[sq])
                    if idx == 0:
                        ssq_cq = PB.next()
                    p.mm(ssq_cq[:, :], onesf[:, :], sq[:, :], idx == 0, idx == 1, r=[onesf, sq], w=[ssq_cq])
                    if idx == 1:
                        rs = FT.next()
                        p.op("act", lambda e, rs=rs, s=ssq_cq: e.activation(rs[:, :], s[:, :], AF.Sqrt, bias=EPS, scale=1.0 / 256), r=[ssq_cq], w=[rs])
                        p.op("dve", lambda e, rs=rs: e.reciprocal(rs[:, :], rs[:, :]), r=[rs], w=[rs])
                        for c2 in range(2):
                            p.op("dve", lambda e, rs=rs, c2=c2: e.scalar_tensor_tensor(cqn[c2][:, :], CV[c2][:, :], vcol(vb + V_GQ + c2), rs[:, :],
                                                                                      ALU.mult, ALU.mult), r=[CV[c2], rs, vecs], w=[cqn[c2]])
                elif kind == "ckv":
                    evac(CV[2][:, :], ps[:, :], [ps], [CV[2]], eng="dve")
                    sq = FT.next()
                    p.op("act", lambda e, sq=sq, ps=ps: e.activation(sq[:, :], ps[:, :], AF.Square), r=[ps], w=[sq])
                    s2 = PB.next()
                    p.mm(s2[:, :], onesf[:, :], sq[:, :], True, True, r=[onesf, sq], w=[s2])
                    rs = FT.next()
                    p.op("act", lambda e, rs=rs, s2=s2: e.activation(rs[:, :], s2[:, :], AF.Sqrt, bias=EPS, scale=1.0 / 128), r=[s2], w=[rs])
                    p.op("dve", lambda e, rs=rs: e.reciprocal(rs[:, :], rs[:, :]), r=[rs], w=[rs])
                    p.op("dve", lambda e, rs=rs: e.scalar_tensor_tensor(ckvn[l][:, t0:t0 + G], CV[2][:, :], vcol(vb + V_GKV), rs[:, :],
                                                                        ALU.mult, ALU.mult), r=[CV[2], rs, vecs], w=[ckvn[l]])
                elif kind == "kp":
                    if idx == 0:
                        kp_t1 = FT.next()
                        p.op("dve", lambda e, t=kp_t1, ps=ps: e.tensor_tensor(t[64:96, :], ps[64:96, :], cosT[64:96, :], ALU.mult),
                             r=[ps, cosT], w=[kp_t1])
                    else:
                        t2 = FT.next()
                        p.op("dve", lambda e, t=t2, ps=ps: e.tensor_tensor(t[64:96, :], ps[64:96, :], sinT[64:96, :], ALU.mult),
                             r=[ps, sinT], w=[t2])
                        p.op("dve", lambda e, t=t2, t1=kp_t1: e.tensor_tensor(kper[l][64:96, t0:t0 + G], t1[64:96, :], t[64:96, :], ALU.add),
                             r=[kp_t1, t2], w=[kper[l]])
            ws_after()
            stage(3.9)
            slot = ws_get("v")
            sv = slot[:, 0:2048].rearrange("p (k m) -> p k m", k=8)
            for t in range(4):
                ps = PB.next()
                for kc in range(8):
                    p.mm(ps[:, 0:256], hT[kc][:, t * 128:(t + 1) * 128], sv[:, kc, :], kc == 0, kc == 7, r=[slot, hT[kc]], w=[ps])
                evac(vxs[l][:, 1 + t, :], ps[:, 0:256], [ps], [vxs[l]])
            ws_after()

            stage(4)
            qv = WKt[:, 8:12, :]
            yAv = WKt[:, 14:18, :]
            for b in range(4):
                gb = g * 4 + b
                for gq in range(2):
                    R = slice(gq * 64, gq * 64 + 64)
                    kts = [0, 1] if gb > 0 else [1]
                    pts = []
                    for kt in kts:
                        ps = PB.next()
                        kc0 = b * 128 + kt * 128
                        p.mm(ps[:, :], kTs[l][R, kc0:kc0 + 128], qv[R, :, b * 128:(b + 1) * 128], True, True,
                             r=[kTs[l]] + qT, w=[ps])
                        sc = FT.next()
                        bv = biasT[:, kt, gq * 4:(gq + 1) * 4, :]
                        p.op("dve", lambda e, sc=sc, ps=ps, bv=bv: e.scalar_tensor_tensor(
                            sc[:, :].rearrange("p (h q) -> p h q", h=4), ps[:, :].rearrange("p (h q) -> p h q", h=4), 0.125, bv,
                            ALU.mult, ALU.add), r=[ps, biasT], w=[sc])
                        pt = PT.next()
                        p.op("act", lambda e, pt=pt, sc=sc: e.activation(pt[:, :], sc[:, :], AF.Exp), r=[sc], w=[pt])
                        pts.append((kt, pt))
                    pso = PB.next()
                    psd = PB.next()
                    for i, (kt, pt) in enumerate(pts):
                        p.mm(pso[:, :], vxs[l][:, b + kt, gq * 128:(gq + 1) * 128], pt[:, :], i == 0, i == len(pts) - 1,
                             r=[vxs[l], pt], w=[pso])
                    for i, (kt, pt) in enumerate(pts):
                        p.mm(psd[:, :], onesb[:, :], pt[:, :], i == 0, i == len(pts) - 1, r=[onesb, pt], w=[psd])
                    den = FT.next()
                    for j in range(4):
                        sc_i = vb + V_SINK + gq * 4 + j
                        p.op("dve", lambda e, den=den, psd=psd, j=j, sc_i=sc_i, R=R: e.tensor_scalar(
                            den[R, j * 128:(j + 1) * 128], psd[R, j * 128:(j + 1) * 128], vecs[R, sc_i:sc_i + 1], None, ALU.add),
                            r=[psd, vecs], w=[den])
                    p.op("dve", lambda e, den=den, R=R: e.reciprocal(den[R, :], den[R, :]), r=[den], w=[den])
                    p.op("dve", lambda e, den=den, pso=pso, R=R, b=b: e.tensor_tensor(
                        yAv[R, :, b * 128:(b + 1) * 128], pso[R, :].rearrange("p (h q) -> p h q", h=4),
                        den[R, :].rearrange("p (h q) -> p h q", h=4), ALU.mult), r=[pso, den], w=yA)
            p.op("pool", lambda e, l=l: e.tensor_copy(kTs[l][:, 0:128], kTs[l][:, G:G + 128]), r=[kTs[l]], w=[kTs[l]])
            p.op("pool", lambda e, l=l: e.tensor_copy(vxs[l][:, 0, :], vxs[l][:, 4, :]), r=[vxs[l]], w=[vxs[l]])

            stage(5)
            if g == 0:
                p.op("pool", lambda e: e.memset(U[:, :, 0:32], 0.0), w=[U])
            else:
                p.op("pool", lambda e, l=l: e.tensor_copy(U[:, :, 0:32], halo[l][:, :, :]), r=[halo[l]], w=[U])
            wd0 = vb + V_WDW
            pss = PAl[0]
            psq = PAl[1]
            conv_ops = []
            for c in range(4):
                acc = CV[c]
                conv_ops.append((lambda c=c, acc=acc: p.op("dve", lambda e: e.tensor_scalar(
                    acc[:, :], U[:, c, 2:2 + G], vcol(wd0 + c * 31), vcol(vb + V_BDW + c), ALU.mult, ALU.add), r=[U, vecs], w=[acc])))
                for j in range(1, CONVW):
                    conv_ops.append((lambda c=c, acc=acc, j=j: p.op("dve", lambda e: e.scalar_tensor_tensor(
                        acc[:, :], U[:, c, 2 + j:2 + j + G], vcol(wd0 + c * 31 + j), acc[:, :], ALU.mult, ALU.add), r=[U, vecs, acc], w=[acc])))
            conv_ops.append(lambda l=l: p.op("pool", lambda e: e.tensor_copy(halo[l][:, :, :], U[:, :, G:G + 32]), r=[U], w=[halo[l]]))

            def conv_finish():
                for c in range(4):
                    acc = CV[c]
                    sq = FT.next()
                    p.op("act", lambda e: e.activation(sq[:, :], acc[:, :], AF.Square), r=[acc], w=[sq])
                    p.mm(pss[:, :], onesf[:, :], acc[:, :], c == 0, c == 3, r=[onesf, acc], w=[pss])
                    p.mm(psq[:, :], onesf[:, :], sq[:, :], c == 0, c == 3, r=[onesf, sq], w=[psq])
            def conv_ln():
                mean = FT.next()
                var = FT.next()
                p.op("act", lambda e, mean=mean: e.activation(mean[:, :], pss[:, :], AF.Copy, scale=1.0 / 512), r=[pss], w=[mean])
                p.op("dve", lambda e, mean=mean, var=var: e.tensor_tensor(var[:, :], mean[:, :], mean[:, :], ALU.mult), r=[mean], w=[var])
                p.op("dve", lambda e, var=var: e.scalar_tensor_tensor(var[:, :], psq[:, :], 1.0 / 512, var[:, :], ALU.mult, ALU.subtract),
                     r=[psq, var], w=[var])
                p.op("act", lambda e, var=var: e.activation(var[:, :], var[:, :], AF.Sqrt, bias=EPS), r=[var], w=[var])
                p.op("dve", lambda e, var=var: e.reciprocal(var[:, :], var[:, :]), r=[var], w=[var])
                for c in range(4):
                    acc = CV[c]
                    p.op("dve", lambda e, acc=acc, mean=mean: e.tensor_tensor(acc[:, :], acc[:, :], mean[:, :], ALU.subtract), r=[acc, mean], w=[acc])
                    p.op("pool", lambda e, acc=acc, var=var: e.tensor_tensor(acc[:, :], acc[:, :], var[:, :], ALU.mult), r=[acc, var], w=[acc])
                    p.op("act", lambda e, acc=acc, c=c: e.activation(yB[c][:, :], acc[:, :], AF.Silu, bias=vcol(vb + V_BLN + c), scale=vcol(vb + V_GLN + c)),
                         r=[acc, vecs], w=[yB[c]])


            stage(6)
            slq = [ws_get("q0"), None]
            slq[1] = ws_get("q1")
            slkv = ws_get("kv")

            def q_prep(h):
                sq_slot = slq[h // 4]
                qv2 = sq_slot[:, 0:2048].rearrange("p (k m) -> p k m", k=2)
                hh = h % 4
                psm = PB.next()
                psp = PB.next()
                for kc in range(2):
                    p.mm(psm[0:96, :], qv2[:, kc, hh * 256:hh * 256 + 96], cqn[kc][:, :], kc == 0, kc == 1, r=[sq_slot, cqn[kc]], w=[psm])
                for kc in range(2):
                    p.mm(psp[0:96, :], qv2[:, kc, hh * 256 + 128:hh * 256 + 224], cqn[kc][:, :], kc == 0, kc == 1, r=[sq_slot, cqn[kc]], w=[psp])
                Qh = QR.next()
                p.op("act", lambda e: e.activation(Qh[0:64, :], psm[0:64, :], AF.Copy), r=[psm], w=[Qh])
                t1 = FT.next()
                t2 = FT.next()
                p.op("dve", lambda e: e.tensor_tensor(t1[64:96, :], psm[64:96, :], cosT[64:96, :], ALU.mult), r=[psm, cosT], w=[t1])
                p.op("dve", lambda e: e.tensor_tensor(t2[64:96, :], psp[64:96, :], sinT[64:96, :], ALU.mult), r=[psp, sinT], w=[t2])
                p.op("pool", lambda e: e.tensor_tensor(Qh[64:96, :], t1[64:96, :], t2[64:96, :], ALU.add), r=[t1, t2], w=[Qh])
                return Qh

            Qnext = q_prep(0)
            per_head = (len(conv_ops) + NH - 1) // NH
            for h in range(NH):
                for cf in conv_ops[h * per_head:(h + 1) * per_head]:
                    cf()
                Qh = Qnext
                if h + 1 < NH:
                    Qnext = q_prep(h + 1)
                PO = PAl[2 + 0] if h % 2 == 0 else PAl[0]
                PD = PAl[2 + 1] if h % 2 == 0 else PAl[1]
                nkt = 4 * (g + 1)
                kv_state = {}

                def prep(j):
                    psk = PB.next()
                    p.mm(psk[0:64, :], slkv[:, h * 192:h * 192 + 64], ckvn[l][:, j * G:(j + 1) * G], True, True, r=[slkv, ckvn[l]], w=[psk])
                    KT = KR.next()
                    evac(KT[0:64, :], psk[0:64, :], [psk], [KT], eng="act")
                    p.op("pool", lambda e: e.tensor_copy(KT[64:96, :], kper[l][64:96, j * G:(j + 1) * G]), r=[kper[l]], w=[KT])
                    psv = PB.next()
                    for t in range(4):
                        tk0 = j * G + t * 128
                        p.mm(psv[:, t * 128:(t + 1) * 128], ckvn[l][:, tk0:tk0 + 128], slkv[:, h * 192 + 64:h * 192 + 192], True, True,
                             r=[slkv, ckvn[l]], w=[psv])
                    Vh = VR.next()
                    evac(Vh[:, :], psv[:, :], [psv], [Vh], eng="dve")
                    kv_state[j] = (KT, Vh)

                def score(kt):
                    j, t = divmod(kt, 4)
                    if j not in kv_state:
                        prep(j)
                    KT, Vh = kv_state[j]
                    q0 = 0 if j < g else t * 128
                    pss2 = PB.next()
                    p.mm(pss2[:, q0:G], KT[0:96, t * 128:(t + 1) * 128], Qh[0:96, q0:G], True, True, r=[KT, Qh], w=[pss2])
                    return pss2

                nxt = score(0)
                for kt in range(nkt):
                    j, t = divmod(kt, 4)
                    q0 = 0 if j < g else t * 128
                    pss2 = nxt
                    if t == 0 and j + 1 <= g and (j + 1) not in kv_state:
                        prep(j + 1)
                    if kt + 1 < nkt:
                        nxt = score(kt + 1)
                    KT, Vh = kv_state[j]
                    pt = PT.next()
                    p.op("act", lambda e: e.activation(pt[:, q0:G], pss2[:, q0:G], AF.Exp, scale=96 ** -0.5), r=[pss2], w=[pt])
                    if j == g:
                        p.op("pool", lambda e: e.tensor_tensor(pt[:, q0:q0 + 128], pt[:, q0:q0 + 128], cmask[:, :], ALU.mult),
                             r=[pt, cmask], w=[pt])
                    p.mm(PO[:, q0:G], Vh[:, t * 128:(t + 1) * 128], pt[:, q0:G], kt == 0, kt == nkt - 1, r=[Vh, pt], w=[PO])
                    p.mm(PD[:, q0:G], onesb[:, :], pt[:, q0:G], kt == 0, kt == nkt - 1, r=[onesb, pt], w=[PD])
                Rh = slice((h % 2) * 64, (h % 2) * 64 + 64)
                rden = FT.next()
                p.op("dve", lambda e, rden=rden, PD=PD, Rh=Rh: e.reciprocal(rden[Rh, :], PD[Rh, :]), r=[PD], w=[rden])
                p.op("dve", lambda e, rden=rden, PO=PO, Rh=Rh, h=h: e.tensor_tensor(yC[h // 2][Rh, :], PO[Rh, :], rden[Rh, :], ALU.mult),
                     r=[PO, rden], w=[yC[h // 2]])

            conv_finish()
            conv_ln()
            ws_after()
            stage(7)
            ys = [yA, yB, yC]
            for dc in range(8):
                sa = ws_get("m%da" % dc)
                sav = sa[:, 0:2048].rearrange("p (k m) -> p k m", k=8)
                sbb = ws_get("m%db" % dc)
                sbg = sbb[:, 0:1024].rearrange("p (k m) -> p k m", k=8)
                sbw = sbb[:, 1024:2048].rearrange("p (n k m) -> p n k m", n=2, k=4)
                scc = ws_get("m%dc" % dc)
                scw = scc[:, 0:512].rearrange("p (k m) -> p k m", k=4)
                acc = FT.next()
                for n in range(3):
                    psg = PB.next()
                    for kc in range(8):
                        lw = sav[:, kc, n * 128:(n + 1) * 128] if n < 2 else sbg[:, kc, :]
                        p.mm(psg[:, :], lw, hT[kc][:, :], kc == 0, kc == 7, r=[sa if n < 2 else sbb, hT[kc]], w=[psg])
                    psb = PB.next()
                    for kc in range(4):
                        lw = sbw[:, n, kc, :] if n < 2 else scw[:, kc, :]
                        p.mm(psb[:, :], lw, ys[n][kc][:, :], kc == 0, kc == 3, r=[sbb if n < 2 else scc, ys[n][kc]], w=[psb])
                    sg = FT.next()
                    p.op("act", lambda e, sg=sg, psg=psg: e.activation(sg[:, :], psg[:, :], AF.Sigmoid), r=[psg], w=[sg])
                    if n == 0:
                        p.op("dve", lambda e, acc=acc, sg=sg, psb=psb: e.tensor_tensor(acc[:, :], psb[:, :], sg[:, :], ALU.mult), r=[psb, sg], w=[acc])
                    else:
                        p.op("dve", lambda e, sg=sg, psb=psb: e.tensor_tensor(sg[:, :], psb[:, :], sg[:, :], ALU.mult), r=[psb, sg], w=[sg])
                        if n == 1:
                            p.op("pool", lambda e, acc=acc, sg=sg: e.tensor_tensor(acc[:, :], acc[:, :], sg[:, :], ALU.add), r=[acc, sg], w=[acc])
                        else:
                            p.op("pool", lambda e, acc=acc, sg=sg, dc=dc: e.tensor_tensor(mg[dc][:, :], acc[:, :], sg[:, :], ALU.add),
                                 r=[acc, sg], w=[mg[dc]])
                ws_after()
            if dbg == "mix":
                continue
            stage(8)
            for oc in range(8):
                if oc % 2 == 0:
                    so = ws_get("o%d" % (oc // 2))
                    sov = so[:, 0:2048].rearrange("p (k m) -> p k m", k=8)
                ps = PB.next()
                for kc in range(8):
                    p.mm(ps[:, :], sov[:, kc, (oc % 2) * 128:(oc % 2) * 128 + 128], mg[kc][:, :], kc == 0, kc == 7, r=[so, mg[kc]], w=[ps])
                p.op("dve", lambda e, ps=ps, oc=oc: e.tensor_tensor(X[oc][:, :], X[oc][:, :], ps[:, :], ALU.add), r=[X[oc], ps], w=[X[oc]])
                if oc % 2 == 1:
                    ws_after()
            stage(9)
            rmsnorm_to(h2, vb + V_GMLP)
            for hc in range(32):
                if hc % 2 == 0:
                    su = ws_get("u%d" % (hc // 2))
                    suv = su[:, 0:2048].rearrange("p (k m) -> p k m", k=8)
                ps = PB.next()
                for kc in range(8):
                    p.mm(ps[:, :], suv[:, kc, (hc % 2) * 128:(hc % 2) * 128 + 128], h2[kc][:, :], kc == 0, kc == 7, r=[su, h2[kc]], w=[ps])
                rl = FT.next()
                p.op("act", lambda e, rl=rl, ps=ps: e.activation(rl[:, :], ps[:, :], AF.Relu), r=[ps], w=[rl])
                eng = "pool" if hc % 2 == 0 else "dve"
                p.op(eng, lambda e, rl=rl, hc=hc: e.tensor_tensor(hid[hc][:, :], rl[:, :], rl[:, :], ALU.mult), r=[rl], w=[hid[hc]])
                if hc % 2 == 1:
                    ws_after()
            for oc in range(8):
                ps = PB.next()
                for half in range(2):
                    sd = ws_get("d%d" % (oc * 2 + half))
                    sdv = sd[:, 0:2048].rearrange("p (k m) -> p k m", k=16)
                    for k2 in range(16):
                        kc = half * 16 + k2
                        p.mm(ps[:, :], sdv[:, k2, :], hid[kc][:, :], kc == 0, kc == 31, r=[sd, hid[kc]], w=[ps])
                    ws_after()
                p.op("dve", lambda e, ps=ps, oc=oc: e.tensor_tensor(X[oc][:, :], X[oc][:, :], ps[:, :], ALU.add), r=[X[oc], ps], w=[X[oc]])
        if dbg is None:
            rmsnorm_to(X, V_GFIN)
        for c in range(8):
            out_ops.append(p.dma("sp", out_d[c * 128:(c + 1) * 128, t0:t0 + G], X[c][:, :], r=[X[c]]))
      except _Stop:
        for c in range(8):
            out_ops.append(p.dma("sp", out_d[c * 128:(c + 1) * 128, g * G:(g + 1) * G], X[c][:, :], r=[X[c]]))
    p.finish(out_ops)
    p.emit()
    return nc, p


def _unit(Wsel):
    K, M = Wsel.shape
    return np.ascontiguousarray(Wsel.reshape(K // 128, 128, M).transpose(1, 0, 2)).reshape(128, -1)


def prep_weights(NL, w_in, w_q_up, w_kv_up, w_branch, w_out, w_up, w_down):
    layers = []
    perm = np.concatenate([np.arange(16, 32), np.arange(0, 16)])
    for l in range(NL):
        Wi = w_in[l]
        cols = []
        for c in range(4):
            cols.append(Wi[:, c * 64:(c + 1) * 64])
            cols.append(Wi[:, (4 + c) * 64:(5 + c) * 64])
        cols.append(Wi[:, 512:640])
        for c in range(4):
            cols.append(Wi[:, 768 + c * 128:768 + (c + 1) * 128])
            cols.append(Wi[:, 1280 + c * 128:1280 + (c + 1) * 128])
        cols.append(Wi[:, 1792:2048])
        cols.append(Wi[:, 2048:2176])
        z64 = np.zeros((D, 64), np.float32)
        kpe = Wi[:, 2176:2208]
        cols += [z64, kpe, z64, kpe[:, perm]]
        fm = np.concatenate(cols, axis=1)
        assert fm.shape[1] == 2240
        parts = []
        for i in range(9):
            parts.append(_unit(fm[:, i * 256:min((i + 1) * 256, 2240)]))
        v0 = Wi[:, 640:704]
        v1 = Wi[:, 704:768]
        parts.append(_unit(np.concatenate([v0, v0, v1, v1], axis=1)))
        wq = w_q_up[l]
        for half in range(2):
            hc = []
            for h in range(half * 4, half * 4 + 4):
                blk = wq[:, h * 96:(h + 1) * 96]
                pe = blk[:, 64:96]
                z32 = np.zeros((256, 32), np.float32)
                z64q = np.zeros((256, 64), np.float32)
                hc += [blk, z32, z64q, pe[:, perm], z32]
            parts.append(_unit(np.concatenate(hc, axis=1)))
        wkv = w_kv_up[l]
        hc = []
        for h in range(8):
            blk = wkv[:, h * 128:(h + 1) * 128]
            hc += [blk[:, 0:64], blk[:, 64:128], blk[:, 64:128]]
        parts.append(_unit(np.concatenate(hc, axis=1)))
        gates = Wi[:, 2208:]
        wb = w_branch[l]
        for dc in range(8):
            g0 = gates[:, 0 * 1024 + dc * 128:0 * 1024 + (dc + 1) * 128]
            g1 = gates[:, 1 * 1024 + dc * 128:1 * 1024 + (dc + 1) * 128]
            g2 = gates[:, 2 * 1024 + dc * 128:2 * 1024 + (dc + 1) * 128]
            parts.append(_unit(np.concatenate([g0, g1], axis=1)))
            wbs = []
            for n in range(3):
                Wn = wb[n]
                if n == 0:
                    rows = []
                    for c in range(4):
                        rows += list(range(c * 64, (c + 1) * 64)) + list(range((4 + c) * 64, (5 + c) * 64))
                    Wn = Wn[rows]
                wbs.append(_unit(Wn[:, dc * 128:(dc + 1) * 128]))
            parts.append(np.concatenate([_unit(g2), wbs[0], wbs[1]], axis=1))
            parts.append(wbs[2])
        for i in range(4):
            parts.append(_unit(w_out[l][:, i * 256:(i + 1) * 256]))
        for i in range(16):
            parts.append(_unit(w_up[l][:, i * 256:(i + 1) * 256]))
        for oc in range(8):
            for half in range(2):
                parts.append(_unit(w_down[l][half * 2048:(half + 1) * 2048, oc * 128:(oc + 1) * 128]))
        layers.append(np.concatenate(parts, axis=1))
    return np.ascontiguousarray(np.stack(layers, 0))


def prep_vecs(NL, g_final, g_mix, g_q_norm, g_kv_norm, w_dw, b_dw, g_conv_ln, b_conv_ln, swa_sinks, g_mlp):
    NV = NL * V_PER_LAYER + 8 + 2
    v = np.zeros((128, NV), np.float32)
    for l in range(NL):
        b = l * V_PER_LAYER
        v[:, b + V_GMIX:b + V_GMIX + 8] = g_mix[l].reshape(8, 128).T
        v[:, b + V_GMLP:b + V_GMLP + 8] = g_mlp[l].reshape(8, 128).T
        v[:, b + V_GQ:b + V_GQ + 2] = g_q_norm[l].reshape(2, 128).T
        v[:, b + V_GKV] = g_kv_norm[l]
        wd = w_dw[l].reshape(CONVW, 4, 128)
        v[:, b + V_WDW:b + V_WDW + 124] = wd.transpose(2, 1, 0).reshape(128, 124)
        v[:, b + V_BDW:b + V_BDW + 4] = b_dw[l].reshape(4, 128).T
        v[:, b + V_GLN:b + V_GLN + 4] = g_conv_ln[l].reshape(4, 128).T
        v[:, b + V_BLN:b + V_BLN + 4] = b_conv_ln[l].reshape(4, 128).T
        v[:, b + V_SINK:b + V_SINK + 8] = swa_sinks[l][None, :]
    o = NL * V_PER_LAYER
    v[:, o:o + 8] = g_final.reshape(8, 128).T
    pp = np.arange(128)
    i = (pp - 64) % 16
    freq = np.exp(np.float32(-math.log(10000.0)) * i.astype(np.float32) / np.float32(16)).astype(np.float32)
    v[:, o + 8] = freq
    v[:, o + 9] = np.where(((pp - 64) % 32) < 16, -1.0, 1.0)
    return v


_CACHE = {}


def kernel(x, positions, rel_bias, g_final, g_mix, w_in, swa_sinks, g_q_norm, w_q_up, g_kv_norm, w_kv_up,
           w_dw, b_dw, g_conv_ln, b_conv_ln, w_branch, w_out, g_mlp, w_up, w_down):
    x = np.asarray(x, np.float32)
    B, S, _ = x.shape
    NL = int(np.asarray(w_in).shape[0])
    f = lambda a: np.asarray(a, np.float32)
    wts = prep_weights(NL, f(w_in), f(w_q_up), f(w_kv_up), f(w_branch), f(w_out), f(w_up), f(w_down))
    vecs = prep_vecs(NL, f(g_final), f(g_mix), f(g_q_norm), f(g_kv_norm), f(w_dw), f(b_dw), f(g_conv_ln),
                     f(b_conv_ln), f(swa_sinks), f(g_mlp))
    relb = f(rel_bias).reshape(1, 256)
    pos = np.asarray(positions, np.int32)
    key = (S, NL)
    if key not in _CACHE:
        _CACHE[key] = build(S, NL)[0]
    nc = _CACHE[key]
    in_maps = []
    for b in range(B):
        in_maps.append({"xT": np.ascontiguousarray(x[b].T), "pos": np.ascontiguousarray(pos[b:b + 1]),
                        "relb": relb, "vecs": vecs, "wts": wts})
    res = run_bass_kernel_spmd(nc, in_maps, core_ids=list(range(B)))
    out = np.stack([np.ascontiguousarray(r["outT"].T) for r in res.results], 0)
    return out.astype(np.float32)
```
